# Optimizing a Trainium2 kernel written in Bass

```python
import jax
import jax.numpy as jnp
from jax import lax
import numpy as np

D_MODEL = 2048
BATCH = 8
SEQ = 2048
DEPTH = 2

MIX_WIDTH = D_MODEL // 2
N_BRANCH = 3
GDN_DK = 128
GDN_DV = 128
GDN_HEADS = MIX_WIDTH // GDN_DV
GDN_CHUNK = 64
LRU_WIDTH = MIX_WIDTH
LRU_BLOCKS = 8
LRU_C = 8.0
GLA_HEADS = 4
GLA_DV = MIX_WIDTH // GLA_HEADS
GLA_DK = GLA_DV // 2
GLA_GATE_RANK = 16
GLA_GATE_NORM = 16.0
GLA_CHUNK = 16
CONV_K = 4
D_FF = 4 * D_MODEL
N_MOD = 6
EPS = 1e-6

GDN_QK_W = GDN_HEADS * GDN_DK
GDN_V_W = GDN_HEADS * GDN_DV
GLA_QK_W = GLA_HEADS * GLA_DK
GLA_V_W = GLA_HEADS * GLA_DV
IN_SPLITS = (GDN_QK_W, GDN_QK_W, GDN_V_W, GDN_V_W, GDN_HEADS, GDN_HEADS,
             LRU_WIDTH, LRU_WIDTH,
             GLA_QK_W, GLA_QK_W, GLA_V_W, GLA_V_W, GLA_GATE_RANK,
             N_BRANCH * D_MODEL)
D_IN = sum(IN_SPLITS)

kernel_name = 'hybrid_gdn_rglru_gla_adaln_block'


def rmsnorm(x, g):
    xf = x.astype(jnp.float32)
    y = xf * lax.rsqrt(jnp.mean(xf * xf, axis=-1, keepdims=True) + EPS)
    return (y * g.astype(jnp.float32)).astype(x.dtype)


def l2norm(x):
    return x * lax.rsqrt(jnp.sum(x * x, axis=-1, keepdims=True) + EPS)


def causal_dwconv(x, w):
    K, C = w.shape
    return lax.conv_general_dilated(x, w[:, None, :].astype(x.dtype), (1,), [(K - 1, 0)],
                                    dimension_numbers=('NWC', 'WIO', 'NWC'),
                                    feature_group_count=C)


def split_heads(x, n_heads):
    B_, T, _ = x.shape
    return x.reshape(B_, T, n_heads, -1).transpose(0, 2, 1, 3)


def to_chunks(x, c):
    return x.reshape(x.shape[:2] + (x.shape[2] // c, c) + x.shape[3:])


def head_norm_gate(o, z, w):
    B_, H, T, d = o.shape
    o = o.transpose(0, 2, 1, 3)
    o = o * lax.rsqrt(jnp.mean(o * o, axis=-1, keepdims=True) + EPS) * w.astype(jnp.float32)
    return (o * jax.nn.silu(z.astype(jnp.float32).reshape(B_, T, H, d))).reshape(B_, T, H * d)


def gated_delta_rule(q, k, v, g, beta):
    C = GDN_CHUNK
    B_, H, T, dk = q.shape
    dv = v.shape[-1]
    q, k, v = (to_chunks(t, C) for t in (q, k, v))
    g, beta = to_chunks(g, C), to_chunks(beta, C)
    gc = jnp.cumsum(g, axis=-1)
    causal = jnp.tril(jnp.ones((C, C), bool))
    strict = jnp.tril(jnp.ones((C, C), bool), -1)
    diff = gc[..., :, None] - gc[..., None, :]
    gamma = jnp.where(causal, jnp.exp(jnp.where(causal, diff, 0.0)), 0.0)
    kb = k * beta[..., None]
    a_kk = jnp.where(strict, jnp.einsum('bhnid,bhnjd->bhnij', kb, k) * gamma, 0.0)
    rhs = jnp.concatenate([v * beta[..., None], kb * jnp.exp(gc)[..., None]], axis=-1)
    sol = lax.linalg.triangular_solve(a_kk + jnp.eye(C, dtype=a_kk.dtype), rhs,
                                      left_side=True, lower=True, unit_diagonal=True)
    u, w = sol[..., :dv], sol[..., dv:]
    a_qk = jnp.where(causal, jnp.einsum('bhnid,bhnjd->bhnij', q, k) * gamma, 0.0)
    q_dec = q * jnp.exp(gc)[..., None]
    k_tail = k * jnp.exp(gc[..., -1:] - gc)[..., None]
    c_dec = jnp.exp(gc[..., -1])

    def step(S, xs):
        u_c, w_c, q_c, k_c, a_c, d_c = xs
        v_new = u_c - jnp.einsum('bhcd,bhde->bhce', w_c, S)
        o = jnp.einsum('bhcd,bhde->bhce', q_c, S) + jnp.einsum('bhij,bhje->bhie', a_c, v_new)
        S = S * d_c[..., None, None] + jnp.einsum('bhcd,bhce->bhde', k_c, v_new)
        return S, o

    xs = tuple(jnp.moveaxis(t, 2, 0) for t in (u, w, q_dec, k_tail, a_qk, c_dec))
    S0 = jnp.zeros((B_, H, dk, dv), q.dtype)
    _, o = lax.scan(step, S0, xs)
    return jnp.moveaxis(o, 0, 2).reshape(B_, H, T, dv)


def gla_chunked(q, k, v, log_a):
    C = GLA_CHUNK
    B_, H, T, dk = q.shape
    dv = v.shape[-1]
    q, k, v, log_a = (to_chunks(t, C) for t in (q, k, v, log_a))
    b = jnp.cumsum(log_a, axis=-2)
    q_in = q * jnp.exp(b)
    k_in = k * jnp.exp(-b)
    k_out = k * jnp.exp(b[..., -1:, :] - b)
    causal = jnp.tril(jnp.ones((C, C), bool))
    a_qk = jnp.where(causal, jnp.einsum('bhnid,bhnjd->bhnij', q_in, k_in), 0.0)
    o_intra = jnp.einsum('bhnij,bhnje->bhnie', a_qk, v)
    c_dec = jnp.exp(b[..., -1, :])

    def step(S, xs):
        q_c, k_c, v_c, d_c = xs
        o = jnp.einsum('bhcd,bhde->bhce', q_c, S)
        S = S * d_c[..., None] + jnp.einsum('bhcd,bhce->bhde', k_c, v_c)
        return S, o

    xs = tuple(jnp.moveaxis(t, 2, 0) for t in (q_in, k_out, v, c_dec))
    S0 = jnp.zeros((B_, H, dk, dv), q.dtype)
    _, o_inter = lax.scan(step, S0, xs)
    return (o_intra + jnp.moveaxis(o_inter, 0, 2)).reshape(B_, H, T, dv)


def _linear_combine(left, right):
    a_l, b_l = left
    a_r, b_r = right
    return a_l * a_r, a_r * b_l + b_r


def rg_lru(x, w_a, b_a, w_i, b_i, lam):
    f32 = jnp.float32
    B_, T, W = x.shape
    xb = x.reshape(B_, T, LRU_BLOCKS, W // LRU_BLOCKS)
    r = jax.nn.sigmoid(jnp.einsum('btki,kij->btkj', xb, w_a.astype(f32)).reshape(B_, T, W) + b_a.astype(f32))
    i = jax.nn.sigmoid(jnp.einsum('btki,kij->btkj', xb, w_i.astype(f32)).reshape(B_, T, W) + b_i.astype(f32))
    log_a = -LRU_C * r * jax.nn.softplus(-lam.astype(f32))
    first = (jnp.arange(T) == 0)[None, :, None]
    mult = jnp.where(first, 1.0, jnp.sqrt(-jnp.expm1(2.0 * log_a)))
    _, hs = lax.associative_scan(_linear_combine, (jnp.exp(log_a), mult * i * x), axis=1)
    return hs


def token_mixer(h, w_in, conv_gdn, gdn_a_log, gdn_dt_bias, gdn_norm, conv_lru, conv_lru_b,
                lru_w_a, lru_b_a, lru_w_i, lru_b_i, lru_lambda, gla_w_gate, gla_b_gate,
                gla_norm, w_branch, w_out):
    f32 = jnp.float32
    split_at = np.cumsum(IN_SPLITS)[:-1].tolist()
    (qa, ka, va, za, beta_a, alpha_a, xb, yb,
     qc, kc, vc, zc, gk_c, gates) = jnp.split(h @ w_in, split_at, axis=-1)

    qkv = jax.nn.silu(causal_dwconv(jnp.concatenate([qa, ka, va], axis=-1), conv_gdn)).astype(f32)
    qa, ka, va = jnp.split(qkv, [GDN_QK_W, 2 * GDN_QK_W], axis=-1)
    qa = l2norm(split_heads(qa, GDN_HEADS)) * (GDN_DK ** -0.5)
    ka = l2norm(split_heads(ka, GDN_HEADS))
    va = split_heads(va, GDN_HEADS)
    beta = jax.nn.sigmoid(beta_a.astype(f32)).transpose(0, 2, 1)
    g = -(jnp.exp(gdn_a_log.astype(f32))
          * jax.nn.softplus(alpha_a.astype(f32) + gdn_dt_bias.astype(f32))).transpose(0, 2, 1)
    oa = head_norm_gate(gated_delta_rule(qa, ka, va, g, beta), za, gdn_norm)

    xl = causal_dwconv(xb, conv_lru) + conv_lru_b
    hl = rg_lru(xl.astype(f32), lru_w_a, lru_b_a, lru_w_i, lru_b_i, lru_lambda)
    ob = jax.nn.gelu(yb.astype(f32)) * hl

    qc = split_heads(qc.astype(f32), GLA_HEADS) * (GLA_DK ** -0.5)
    kc = split_heads(kc.astype(f32), GLA_HEADS)
    vc = split_heads(vc.astype(f32), GLA_HEADS)
    log_a = jax.nn.log_sigmoid((gk_c @ gla_w_gate + gla_b_gate).astype(f32)) / GLA_GATE_NORM
    oc = head_norm_gate(gla_chunked(qc, kc, vc, split_heads(log_a, GLA_HEADS)), zc, gla_norm)

    gates = jax.nn.sigmoid(gates.astype(f32)).reshape(h.shape[:2] + (N_BRANCH, D_MODEL))
    merged = 0.0
    for n, o in enumerate((oa, ob, oc)):
        merged = merged + gates[:, :, n] * (o.astype(h.dtype) @ w_branch[n])
    return merged.astype(h.dtype) @ w_out


def sq_relu_mlp(h, w1, w2):
    return jnp.square(jax.nn.relu(h @ w1)) @ w2


def setup_inputs(seed: int = 0) -> dict:
    key = jax.random.key(seed)
    ks = iter(jax.random.split(key, 40))

    def nrm(shape, std):
        return std * jax.random.normal(next(ks), shape, jnp.float32)

    def unif(shape, lo, hi):
        return jax.random.uniform(next(ks), shape, jnp.float32, lo, hi)

    L = DEPTH
    x = nrm((BATCH, SEQ, D_MODEL), 1.0)
    c = nrm((BATCH, D_MODEL), 1.0)
    w_ada = nrm((L, D_MODEL, N_MOD * D_MODEL), 0.5 * D_MODEL ** -0.5)
    b_ada = nrm((L, N_MOD * D_MODEL), 0.01)
    g_pre_mix = 1.0 + nrm((L, D_MODEL), 0.05)
    g_post_mix = 1.0 + nrm((L, D_MODEL), 0.05)
    g_pre_mlp = 1.0 + nrm((L, D_MODEL), 0.05)
    g_post_mlp = 1.0 + nrm((L, D_MODEL), 0.05)
    w_in = nrm((L, D_MODEL, D_IN), D_MODEL ** -0.5)
    conv_gdn = nrm((L, CONV_K, 2 * GDN_QK_W + GDN_V_W), CONV_K ** -0.5)
    gdn_a_log = jnp.log(unif((L, GDN_HEADS), 1.0, 16.0))
    dt = jnp.exp(unif((L, GDN_HEADS), float(np.log(1e-3)), float(np.log(1e-1))))
    gdn_dt_bias = dt + jnp.log(-jnp.expm1(-dt))
    gdn_norm = 1.0 + nrm((L, GDN_DV), 0.05)
    conv_lru = nrm((L, CONV_K, LRU_WIDTH), CONV_K ** -0.5)
    conv_lru_b = nrm((L, LRU_WIDTH), 0.01)
    blk = LRU_WIDTH // LRU_BLOCKS
    lru_w_a = nrm((L, LRU_BLOCKS, blk, blk), blk ** -0.5)
    lru_b_a = nrm((L, LRU_WIDTH), 0.01)
    lru_w_i = nrm((L, LRU_BLOCKS, blk, blk), blk ** -0.5)
    lru_b_i = nrm((L, LRU_WIDTH), 0.01)
    rho = unif((L, LRU_WIDTH), 0.9, 0.999)
    s = rho ** (1.0 / LRU_C)
    lru_lambda = jnp.log(s) - jnp.log1p(-s)
    gla_w_gate = nrm((L, GLA_GATE_RANK, GLA_QK_W), GLA_GATE_RANK ** -0.5)
    gla_b_gate = nrm((L, GLA_QK_W), 0.01)
    gla_norm = 1.0 + nrm((L, GLA_DV), 0.05)
    w_branch = nrm((L, N_BRANCH, MIX_WIDTH, D_MODEL), MIX_WIDTH ** -0.5)
    w_out = nrm((L, D_MODEL, D_MODEL), D_MODEL ** -0.5)
    w_mlp1 = nrm((L, D_MODEL, D_FF), D_MODEL ** -0.5)
    w_mlp2 = nrm((L, D_FF, D_MODEL), D_FF ** -0.5)
    return {'x': x, 'c': c, 'w_ada': w_ada, 'b_ada': b_ada,
            'g_pre_mix': g_pre_mix, 'g_post_mix': g_post_mix,
            'g_pre_mlp': g_pre_mlp, 'g_post_mlp': g_post_mlp,
            'w_in': w_in, 'conv_gdn': conv_gdn, 'gdn_a_log': gdn_a_log,
            'gdn_dt_bias': gdn_dt_bias, 'gdn_norm': gdn_norm,
            'conv_lru': conv_lru, 'conv_lru_b': conv_lru_b,
            'lru_w_a': lru_w_a, 'lru_b_a': lru_b_a, 'lru_w_i': lru_w_i, 'lru_b_i': lru_b_i,
            'lru_lambda': lru_lambda, 'gla_w_gate': gla_w_gate, 'gla_b_gate': gla_b_gate,
            'gla_norm': gla_norm, 'w_branch': w_branch, 'w_out': w_out,
            'w_mlp1': w_mlp1, 'w_mlp2': w_mlp2}


def reference(x, c, w_ada, b_ada, g_pre_mix, g_post_mix, g_pre_mlp, g_post_mlp,
              w_in, conv_gdn, gdn_a_log, gdn_dt_bias, gdn_norm, conv_lru, conv_lru_b,
              lru_w_a, lru_b_a, lru_w_i, lru_b_i, lru_lambda, gla_w_gate, gla_b_gate,
              gla_norm, w_branch, w_out, w_mlp1, w_mlp2):
    for l in range(DEPTH):
        mod = jax.nn.silu(c) @ w_ada[l] + b_ada[l]
        sh1, sc1, gt1, sh2, sc2, gt2 = jnp.split(mod[:, None, :], N_MOD, axis=-1)
        h = rmsnorm(x, g_pre_mix[l]) * (1.0 + sc1) + sh1
        y = token_mixer(h, w_in[l], conv_gdn[l], gdn_a_log[l], gdn_dt_bias[l], gdn_norm[l],
                        conv_lru[l], conv_lru_b[l], lru_w_a[l], lru_b_a[l], lru_w_i[l], lru_b_i[l],
                        lru_lambda[l], gla_w_gate[l], gla_b_gate[l], gla_norm[l],
                        w_branch[l], w_out[l])
        x = x + gt1 * rmsnorm(y, g_post_mix[l])
        h = rmsnorm(x, g_pre_mlp[l]) * (1.0 + sc2) + sh2
        x = x + gt2 * rmsnorm(sq_relu_mlp(h, w_mlp1[l], w_mlp2[l]), g_post_mlp[l])
    return x
```

```python
import numpy as np
from contextlib import ExitStack
import concourse.bass as bass
import concourse.mybir as mybir
from concourse.bass_utils import run_bass_kernel_spmd

F32 = mybir.dt.float32
BF16 = mybir.dt.bfloat16
AF = mybir.ActivationFunctionType
ALU = mybir.AluOpType
AX = mybir.AxisListType

T = 2048
D = 2048
NT = T // 128
DIN = 15392
DFF = 8192
EPS = 1e-6
C_QA, C_KA, C_VA, C_ZA, C_BETA, C_ALPHA, C_XB, C_YB, C_QC, C_KC, C_VC, C_ZC, C_GK, C_GATES = (
    0, 1024, 2048, 3072, 4096, 4104, 4112, 5136, 6160, 6672, 7184, 8208, 9232, 9248)
BIG = 1.0e30


class Sem:
    __slots__ = ("h", "val")

    def __init__(self, h):
        self.h = h
        self.val = 0


class Buf:
    __slots__ = ("t", "lastw", "readers", "sem_in", "sem_out")

    def __init__(self, t):
        self.t = t
        self.lastw = {}
        self.readers = {}
        self.sem_in = None
        self.sem_out = None

    def __getitem__(self, k):
        return self.t[k]


def _merge(d, tok):
    for s, v in tok.items():
        if d.get(s, 0) < v:
            d[s] = v


class Prog:
    ENGS = ("sp", "act", "dve", "pool", "pe")

    def __init__(self, nc, es):
        self.nc = nc
        self.es = es
        self.free = []
        self.nsem = 0
        self.ops = {e: [] for e in self.ENGS}
        self.esem = {}
        self.waited = {e: {} for e in self.ENGS}
        self.pending = {}
        self.bar = self.new_sem()
        self.fresh_engine_sems()

    def new_sem(self):
        if self.free:
            return self.free.pop()
        self.nsem += 1
        return Sem(self.es.enter_context(self.nc.semaphore("s%d" % self.nsem)))

    def fresh_engine_sems(self):
        for e in self.ENGS:
            self.esem[e] = self.new_sem()

    def buf(self, t):
        return Buf(t)

    def _deps(self, eng, reads, writes):
        deps = {}
        for b in reads:
            _merge(deps, b.lastw)
        for b in writes:
            _merge(deps, b.lastw)
            _merge(deps, b.readers)
        waits = []
        w = self.waited[eng]
        for s, v in deps.items():
            if w.get(s, 0) < v:
                waits.append((s, v))
                w[s] = v
        return waits

    def _commit(self, tok, reads, writes):
        for b in reads:
            _merge(b.readers, tok)
        for b in writes:
            _merge(b.lastw, tok)
            b.readers = {}

    def op(self, eng, fn, reads=(), writes=()):
        waits = self._deps(eng, reads, writes)
        s = self.esem[eng]
        s.val += 1
        assert s.val < 60000
        tok = {s: s.val}
        self.ops[eng].append((waits, fn, s, 1))
        self._commit(tok, reads, writes)
        return tok

    def dma(self, q, out, in_, reads=(), writes=(), **kw):
        waits = self._deps(q, reads, writes)
        if writes:
            b = writes[0]
            if b.sem_in is None:
                b.sem_in = self.new_sem()
            s = b.sem_in
        else:
            b = reads[0]
            if b.sem_out is None:
                b.sem_out = self.new_sem()
            s = b.sem_out
        s.val += 16
        assert s.val < 60000
        tok = {s: s.val}
        self.ops[q].append((waits, lambda e: e.dma_start(out=out, in_=in_, **kw), s, 16))
        self._commit(tok, reads, writes)
        _merge(self.pending, tok)
        return tok

    def release(self, bufs):
        for b in bufs:
            for s in (b.sem_in, b.sem_out):
                if s is not None:
                    self.free.append(s)
            b.sem_in = b.sem_out = None

    def barrier(self):
        deps = dict(self.pending)
        for e in self.ENGS:
            s = self.esem[e]
            if s.val:
                deps[s] = max(deps.get(s, 0), s.val)
        w = self.waited["sp"]
        waits = [(s, v) for s, v in deps.items() if w.get(s, 0) < v]
        self.bar.val += 1
        bv = self.bar.val
        bar = self.bar
        self.ops["sp"].append((waits, lambda e: e.sem_inc(bar.h, 1), None, 0))
        for e in self.ENGS:
            if e != "sp":
                self.ops[e].append(([(bar, bv)], None, None, 0))
            for s, v in deps.items():
                self.waited[e][s] = max(self.waited[e].get(s, 0), v)
        self.pending = {}

    def emit(self):
        nc = self.nc
        with nc.Block() as block:
            table = (("sp", block.sync), ("act", block.scalar), ("dve", block.vector),
                     ("pool", block.gpsimd), ("pe", block.tensor))
            for name, deco in table:
                ops = self.ops[name]

                def body(e, ops=ops):
                    for waits, fn, s, inc in ops:
                        for ws, wv in waits:
                            e.wait_ge(ws.h, wv)
                        if fn is None:
                            continue
                        ins = fn(e)
                        if s is not None:
                            ins.then_inc(s.h, inc)
                deco(body)
                self.ops[name] = []


def _consts():
    p = np.arange(128)[:, None]
    f = np.arange(128)[None, :]
    mats = []
    mats.append((p == f).astype(np.float32))
    mats.append((p <= f).astype(np.float32))
    mats.append(np.where(p <= f, 0.0, -BIG).astype(np.float32))
    mats.append(np.where(f < p, 0.0, BIG).astype(np.float32))
    mats.append(((p // 2 == f // 2) & (p % 2 == 1) & (f % 2 == 0)).astype(np.float32))
    for l in range(2, 8):
        s = 2 ** (l - 1)
        mats.append(((p // (2 * s) == f // (2 * s)) & (p % (2 * s) < s) & (f % (2 * s) >= s)).astype(np.float32))
    mats.append(np.ones((128, 128), np.float32))
    return np.concatenate(mats, axis=1)


NCONST = 12


class K:
    def __init__(self, debug=()):
        self.debug = set(debug)
        self.nc = bass.Bass("TRN2", target_bir_lowering=False)
        self.es = ExitStack()
        self.P = Prog(self.nc, self.es)
        self.uid = 0
        self.inp = {}
        self.outs = []

    def din(self, name, shape, dt=F32):
        a = self.nc.dram_tensor(name, list(shape), dt, kind="ExternalInput").ap()
        self.inp[name] = a
        return a

    def dscr(self, name, shape, dt):
        kind = "ExternalOutput" if name in self.debug else "Internal"
        if kind == "ExternalOutput":
            self.outs.append(name)
        return self.nc.dram_tensor(name, list(shape), dt, kind=kind).ap()

    def sb(self, st, shape, dt, name=None):
        self.uid += 1
        t = st.enter_context(self.nc.sbuf_tensor("%s_%d" % (name or "t", self.uid), list(shape), dt))
        b = Buf(t)
        self.live.append(b)
        return b

    def ps(self, st, shape, dt, name=None):
        self.uid += 1
        t = st.enter_context(self.nc.psum_tensor("%s_%d" % (name or "p", self.uid), list(shape), dt))
        b = Buf(t)
        self.live.append(b)
        return b

    def phase_begin(self):
        self.live = []
        return ExitStack()

    def phase_end(self, st):
        self.P.barrier()
        self.P.emit()
        self.P.release(self.live)
        st.close()


def build(debug=(), nlayers=2, stop=None):
    k = K(debug)
    nc, P = k.nc, k.P
    x_in = k.din("x", [T, D])
    c_t = k.din("c_t", [128, 16])
    consts_d = k.din("consts", [128, NCONST * 128])
    w_ada = k.din("w_ada", [2, D, 6 * D]); b_ada = k.din("b_ada", [2, 6 * D])
    g_pre_mix = k.din("g_pre_mix", [2, D]); g_post_mix = k.din("g_post_mix", [2, D])
    g_pre_mlp = k.din("g_pre_mlp", [2, D]); g_post_mlp = k.din("g_post_mlp", [2, D])
    w_in = k.din("w_in", [2, D, DIN])
    conv_gdn_t = k.din("conv_gdn_t", [2, 128, 24, 4])
    gdn_a_log = k.din("gdn_a_log", [2, 8]); gdn_dt_bias = k.din("gdn_dt_bias", [2, 8]); gdn_norm = k.din("gdn_norm", [2, 128])
    conv_lru_t = k.din("conv_lru_t", [2, 128, 8, 4]); lru_vec_t = k.din("lru_vec_t", [2, 128, 4, 8])
    lru_w_a = k.din("lru_w_a", [2, 8, 128, 128]); lru_w_i = k.din("lru_w_i", [2, 8, 128, 128])
    gla_w_gate = k.din("gla_w_gate", [2, 16, 512]); gla_b_gate_t = k.din("gla_b_gate_t", [2, 128, 4]); gla_norm = k.din("gla_norm", [2, 256])
    w_branch = k.din("w_branch", [2, 3, 1024, D]); w_out = k.din("w_out", [2, D, D])
    w_mlp1 = k.din("w_mlp1", [2, D, DFF]); w_mlp2 = k.din("w_mlp2", [2, DFF, D])
    out = nc.dram_tensor("out", [T, D], F32, kind="ExternalOutput").ap()
    k.outs.append("out")
    rows = k.dscr("rows", [2, 6, D], F32)
    xs0 = k.dscr("xs0", [T, D], F32)
    s_qT = k.dscr("s_qT", [1024, T], BF16); s_kT = k.dscr("s_kT", [1024, T], BF16); s_vT = k.dscr("s_vT", [1024, T], BF16)
    s_za = k.dscr("s_za", [T, 1024], BF16)
    s_oT = k.dscr("s_oT", [3, 1024, T], BF16)
    s_gq = k.dscr("s_gq", [512, T], BF16); s_gki = k.dscr("s_gki", [512, T], BF16); s_gko = k.dscr("s_gko", [512, T], BF16)
    s_gv = k.dscr("s_gv", [T, 1024], BF16); s_gz = k.dscr("s_gz", [T, 1024], BF16)
    s_gates = k.dscr("s_gates", [6144, T], BF16)
    s_hdbg = k.dscr("s_hdbg", [D, T], BF16) if "s_hdbg" in k.debug else None

    pst = k.phase_begin()
    cF = k.sb(pst, [128, NCONST * 128], F32, "cF")
    cB = k.sb(pst, [128, NCONST * 128], BF16, "cB")
    betaS = k.sb(pst, [128, NT, 8], F32, "betaS")
    gS = k.sb(pst, [128, NT, 8], F32, "gS")
    dcS = k.sb(pst, [128, 4, NT], F32, "dcS")
    persist = k.live

    def CF(i):
        return cF[:, i * 128:(i + 1) * 128]

    def CB(i):
        return cB[:, i * 128:(i + 1) * 128]

    P.dma("sp", cF[:], consts_d, writes=[cF])
    P.op("dve", lambda e: e.tensor_copy(out=cB[:], in_=cF[:]), reads=[cF], writes=[cB])

    def mod_gen(l, st, nwb):
        csrc = k.sb(st, [128, 16], F32); csil = k.sb(st, [128, 16], BF16)
        vec = [k.sb(st, [1, D], F32, "vec%d" % i) for i in range(6)]
        bad = [k.sb(st, [1, D], F32, "bad%d" % i) for i in range(2)]
        gv = [k.sb(st, [1, D], F32, "gv%d" % i) for i in range(4)]
        wb = [k.sb(st, [128, 16, 512], BF16, "wbm%d" % i) for i in range(nwb)]
        pp = [k.ps(st, [128, 512], F32, "pp%d" % i) for i in range(2)]
        P.dma("sp", csrc[:], c_t, writes=[csrc])
        P.op("act", lambda e: e.activation(out=csil[:], in_=csrc[:], func=AF.Silu), reads=[csrc], writes=[csil])
        for i, gsrc in enumerate((g_pre_mix, g_post_mix, g_pre_mlp, g_post_mlp)):
            P.dma("sp", gv[i][:], gsrc[l:l + 1, :], writes=[gv[i]])
        n = 0
        for m in range(6):
            bd = bad[m % 2]
            P.dma("sp", bd[:], b_ada[l:l + 1, m * D:(m + 1) * D], writes=[bd])
            for nch in range(4):
                w = wb[n % nwb]; p_ = pp[n % 2]; n += 1
                c0 = m * D + nch * 512
                P.dma("pool", w[:], w_ada[l, :, c0:c0 + 512].rearrange("(kc p) n -> p kc n", p=128), writes=[w])

                def mm(e, w=w, p_=p_):
                    for kc in range(16):
                        ins = e.matmul(p_[0:1, :], lhsT=csil[:, kc:kc + 1], rhs=w[:, kc, :], start=(kc == 0), stop=(kc == 15))
                    return ins
                P.op("pe", mm, reads=[w, csil], writes=[p_])
                P.op("dve", lambda e, p_=p_, m=m, nch=nch, bd=bd: e.tensor_tensor(
                    out=vec[m][0:1, nch * 512:(nch + 1) * 512], in0=p_[0:1, :], in1=bd[0:1, nch * 512:(nch + 1) * 512], op=ALU.add),
                    reads=[p_, bd], writes=[vec[m]])
                yield
        sh1, sc1, gt1, sh2, sc2, gt2 = vec
        combos = [(sc1, gv[0], True), (sh1, None, False), (gt1, gv[1], False),
                  (sc2, gv[2], True), (sh2, None, False), (gt2, gv[3], False)]
        for i, (a, g, plus1) in enumerate(combos):
            if g is None:
                pass
            elif plus1:
                P.op("dve", lambda e, a=a, g=g: e.scalar_tensor_tensor(out=a[:], in0=a[:], scalar=1.0, in1=g[:], op0=ALU.add, op1=ALU.mult),
                     reads=[a, g], writes=[a])
            else:
                P.op("dve", lambda e, a=a, g=g: e.tensor_tensor(out=a[:], in0=a[:], in1=g[:], op=ALU.mult), reads=[a, g], writes=[a])
            P.dma("sp", rows[l, i:i + 1, :], a[:], reads=[a])

    def phase_mod(l):
        st = k.phase_begin()
        for _ in mod_gen(l, st, 3):
            pass
        k.phase_end(st)

    def rstd(st_bufs, src, ss, rs, scale_inv_n):
        P.op("dve", lambda e: e.tensor_scalar(out=rs[:], in0=ss[:], scalar1=scale_inv_n, scalar2=EPS, op0=ALU.mult, op1=ALU.add),
             reads=[ss], writes=[rs])
        P.op("act", lambda e: e.activation(out=rs[:], in_=rs[:], func=AF.Sqrt), reads=[rs], writes=[rs])
        P.op("dve", lambda e: e.reciprocal(out=rs[:], in_=rs[:]), reads=[rs], writes=[rs])

    def wload(wb, src2d, kcs, ncols):
        P.dma("pool", wb[:, 0:kcs, 0:ncols], src2d.rearrange("(kc p) n -> p kc n", p=128), writes=[wb])

    def mmB(ps, M, W, c, act, t0, n=512, KC=16):
        def f(e):
            for kc in range(KC):
                ins = e.matmul(ps[0:M, 0:n], lhsT=W[:, kc, c:c + M], rhs=act[:, kc, t0:t0 + n], start=(kc == 0), stop=(kc == KC - 1))
            return ins
        P.op("pe", f, reads=[W, act], writes=[ps])

    def mmA(ps, W, n, act, t0, KC=16):
        def f(e):
            for kc in range(KC):
                ins = e.matmul(ps[:, 0:n], lhsT=act[:, kc, t0:t0 + 128], rhs=W[:, kc, 0:n], start=(kc == 0), stop=(kc == KC - 1))
            return ins
        P.op("pe", f, reads=[W, act], writes=[ps])

    def actf(out, in_, func, reads, writes, **kw):
        P.op("act", lambda e: e.activation(out=out, in_=in_, func=func, **kw), reads=reads, writes=writes)

    def tt(out, in0, in1, op, reads, writes, eng="dve"):
        P.op(eng, lambda e: e.tensor_tensor(out=out, in0=in0, in1=in1, op=op), reads=reads, writes=writes)

    def stt(out, in0, scalar, in1, op0, op1, reads, writes):
        P.op("dve", lambda e: e.scalar_tensor_tensor(out=out, in0=in0, scalar=scalar, in1=in1, op0=op0, op1=op1), reads=reads, writes=writes)

    def ts(out, in0, s1, s2, op0, op1, reads, writes):
        P.op("dve", lambda e: e.tensor_scalar(out=out, in0=in0, scalar1=s1, scalar2=s2, op0=op0, op1=op1), reads=reads, writes=writes)

    def ts1(out, in0, s1, op0, reads, writes):
        P.op("dve", lambda e: e.tensor_scalar(out=out, in0=in0, scalar1=s1, scalar2=None, op0=op0), reads=reads, writes=writes)

    def cpy(eng, out, in_, reads, writes):
        if eng == "act":
            P.op("act", lambda e: e.copy(out=out, in_=in_), reads=reads, writes=writes)
        else:
            P.op("dve", lambda e: e.tensor_copy(out=out, in_=in_), reads=reads, writes=writes)

    def mset(out, val, writes):
        P.op("dve", lambda e: e.memset(out, val), writes=writes)

    def phase_inproj(l, st, hT):
        wb = [k.sb(st, [128, 16, 512], BF16, "wb%d" % i) for i in range(3)]
        wsm = k.sb(st, [128, 16, 16], BF16, "wsm")
        Fb = [k.sb(st, [128, 2051], F32, "F%d" % i) for i in range(6)]
        Hb = [k.sb(st, [128, T], BF16, "H%d" % i) for i in range(4)]
        pp = [k.ps(st, [128, 512], F32, "pp%d" % i) for i in range(6)]
        cw = k.sb(st, [128, 24, 4], F32, "cw"); cwl = k.sb(st, [128, 8, 4], F32, "cwl"); lv = k.sb(st, [128, 4, 8], F32, "lv")
        small = k.sb(st, [128, 64], F32, "small")
        epsT = k.sb(st, [128, 1], F32, "epsT")
        wgi = [k.sb(st, [128, 128], BF16, "wgi%d" % i) for i in range(2)]
        nW = [0]; nP = [0]

        def nextw():
            nW[0] += 1
            return wb[nW[0] % 2]

        def nextp():
            nP[0] += 1
            return pp[nP[0] % 6]
        wl = w_in[l]
        P.dma("sp", cw[:], conv_gdn_t[l], writes=[cw])
        P.dma("sp", cwl[:], conv_lru_t[l], writes=[cwl])
        P.dma("sp", lv[:], lru_vec_t[l], writes=[lv])
        mset(epsT[:], EPS, [epsT])
        for f_ in Fb:
            mset(f_[:, 0:3], 0.0, [f_])

        def proj_rows(dst, W, c, act_eng_toggle=[0]):
            for tb in range(4):
                p_ = nextp()
                mmB(p_, 128, W, c, hT, tb * 512)
                act_eng_toggle[0] += 1
                cpy("act" if act_eng_toggle[0] % 2 else "dve", dst[:, 3 + tb * 512:3 + (tb + 1) * 512], p_[:, :], [p_], [dst])

        def conv4(acc, raw, wv, bias=None):
            if bias is None:
                ts1(acc[:, 3:3 + T], raw[:, 0:T], wv[:, 0:1], ALU.mult, [raw], [acc])
            else:
                ts(acc[:, 3:3 + T], raw[:, 0:T], wv[:, 0:1], bias, ALU.mult, ALU.add, [raw], [acc])
            for j in range(1, 4):
                stt(acc[:, 3:3 + T], raw[:, j:j + T], wv[:, j:j + 1], acc[:, 3:3 + T], ALU.mult, ALU.add, [raw, acc], [acc])

        GBs = [k.sb(st, [128, T], BF16, "GB%d" % i) for i in range(2)]

        def gates_gen():
            for wt in range(12):
                W = wb[2]
                wload(W, wl[:, C_GATES + wt * 512:C_GATES + (wt + 1) * 512], 16, 512)
                for cg in range(4):
                    GB = GBs[(wt * 4 + cg) % 2]
                    for tb in range(4):
                        p_ = nextp()
                        mmB(p_, 128, W, cg * 128, hT, tb * 512)
                        actf(GB[:, tb * 512:(tb + 1) * 512], p_[:, :], AF.Sigmoid, [p_], [GB])
                    r0 = (wt * 4 + cg) * 128
                    P.dma("sp", s_gates[r0:r0 + 128, :], GB[:], reads=[GB])
                    yield
        gg = gates_gen()

        def fill(n):
            for _ in range(n):
                try:
                    next(gg)
                except StopIteration:
                    return

        nh = 0
        for f in range(3):
            for g4 in range(2):
                W = nextw()
                wload(W, wl[:, f * 1024 + g4 * 512: f * 1024 + (g4 + 1) * 512], 16, 512)
                for cg in range(4):
                    head = g4 * 4 + cg
                    cgi = f * 8 + head
                    raw, acc, rsb = Fb[nh % 2], Fb[2 + nh % 2], Fb[4]
                    sil = acc
                    ob = Hb[nh % 2]; sq = Hb[2 + nh % 2]; nh += 1
                    proj_rows(raw, W, cg * 128)
                    fill(1)
                    conv4(acc, raw, cw[:, cgi, :])
                    if f == 2:
                        actf(ob[:], acc[:, 3:3 + T], AF.Silu, [acc], [ob])
                        P.dma("sp", s_vT[head * 128:(head + 1) * 128, :], ob[:], reads=[ob])
                    else:
                        actf(sil[:, 3:3 + T], acc[:, 3:3 + T], AF.Silu, [acc], [sil])
                        actf(sq[:], sil[:, 3:3 + T], AF.Square, [sil], [sq])
                        for tb in range(4):
                            p_ = nextp()
                            P.op("pe", lambda e, p_=p_, sq=sq, tb=tb: e.matmul(p_[:, :], lhsT=CB(11), rhs=sq[:, tb * 512:(tb + 1) * 512], start=True, stop=True),
                                 reads=[sq, cB], writes=[p_])
                            actf(rsb[:, 3 + tb * 512:3 + (tb + 1) * 512], p_[:, :], AF.Ln, [p_, epsT], [rsb], bias=epsT[:, 0:1])
                        actf(rsb[:, 3:3 + T], rsb[:, 3:3 + T], AF.Exp, [rsb], [rsb], scale=-0.5)
                        scl = (128.0 ** -0.5) if f == 0 else 1.0
                        stt(ob[:], sil[:, 3:3 + T], scl, rsb[:, 3:3 + T], ALU.mult, ALU.mult, [sil, rsb], [ob])
                        dst = s_qT if f == 0 else s_kT
                        P.dma("sp", dst[head * 128:(head + 1) * 128, :], ob[:], reads=[ob])

        def fam_tokmajor(c0, ncol, dst, func):
            for wt in range(ncol // 512):
                W = nextw()
                wload(W, wl[:, c0 + wt * 512:c0 + (wt + 1) * 512], 16, 512)
                for t in range(NT):
                    p_ = nextp()
                    mmA(p_, W, 512, hT, t * 128)
                    zb = Hb[t % 4]
                    actf(zb[:, 0:512], p_[:, :], func, [p_], [zb])
                    P.dma("sp", dst[t * 128:(t + 1) * 128, wt * 512:(wt + 1) * 512], zb[:, 0:512], reads=[zb])
        fam_tokmajor(C_ZA, 1024, s_za, AF.Silu)
        fam_tokmajor(C_VC, 1024, s_gv, AF.Copy)
        fam_tokmajor(C_ZC, 1024, s_gz, AF.Silu)

        P.dma("pool", wsm[:], wl[:, C_BETA:C_BETA + 16].rearrange("(kc p) n -> p kc n", p=128), writes=[wsm])
        alog = small[:, 0:8]; dtb = small[:, 8:16]; negA = small[:, 16:24]; tmp8 = small[:, 24:32]
        P.dma("sp", alog, gdn_a_log[l, :].partition_broadcast(128), writes=[small])
        P.dma("sp", dtb, gdn_dt_bias[l, :].partition_broadcast(128), writes=[small])
        actf(negA, alog, AF.Exp, [small], [small])
        ts1(negA, negA, -1.0, ALU.mult, [small], [small])
        for t in range(NT):
            p_ = nextp()
            mmA(p_, wsm, 16, hT, t * 128)
            actf(betaS[:, t, :], p_[:, 0:8], AF.Sigmoid, [p_], [betaS])
            tt(tmp8, p_[:, 8:16], dtb, ALU.add, [p_, small], [small])
            actf(tmp8, tmp8, AF.Exp, [small], [small])
            actf(tmp8, tmp8, AF.Ln, [small], [small], bias=1.0)
            tt(gS[:, t, :], tmp8, negA, ALU.mult, [small], [gS])

        cneg = small[:, 32:40]
        actf(cneg, lv[:, 3, :], AF.Exp, [lv], [small], scale=-1.0)
        actf(cneg, cneg, AF.Ln, [small], [small], bias=1.0)
        ts1(cneg, cneg, -8.0, ALU.mult, [small], [small])
        for j in range(2):
            Wx = nextw(); wload(Wx, wl[:, C_XB + j * 512:C_XB + (j + 1) * 512], 16, 512)
            Wy = nextw(); wload(Wy, wl[:, C_YB + j * 512:C_YB + (j + 1) * 512], 16, 512)
            for q in range(4):
                kb = j * 4 + q
                xraw, yraw, xl, r_, ig, a2 = Fb
                xlb = Hb[0]; OB = Hb[1 + kb % 2]
                P.dma("pool", wgi[0][:], lru_w_a[l, kb], writes=[wgi[0]])
                P.dma("pool", wgi[1][:], lru_w_i[l, kb], writes=[wgi[1]])
                proj_rows(xraw, Wx, q * 128)
                proj_rows(yraw, Wy, q * 128)
                fill(2)
                conv4(xl, xraw, cwl[:, kb, :], bias=lv[:, 0, kb:kb + 1])
                cpy("act", xlb[:], xl[:, 3:3 + T], [xl], [xlb])
                for tb in range(4):
                    sl = slice(3 + tb * 512, 3 + (tb + 1) * 512)
                    p_ = nextp()
                    P.op("pe", lambda e, p_=p_, tb=tb: e.matmul(p_[:, :], lhsT=wgi[0][:], rhs=xlb[:, tb * 512:(tb + 1) * 512], start=True, stop=True),
                         reads=[wgi[0], xlb], writes=[p_])
                    actf(r_[:, sl], p_[:, :], AF.Sigmoid, [p_, lv], [r_], bias=lv[:, 1, kb:kb + 1])
                    p2 = nextp()
                    P.op("pe", lambda e, p2=p2, tb=tb: e.matmul(p2[:, :], lhsT=wgi[1][:], rhs=xlb[:, tb * 512:(tb + 1) * 512], start=True, stop=True),
                         reads=[wgi[1], xlb], writes=[p2])
                    actf(ig[:, sl], p2[:, :], AF.Sigmoid, [p2, lv], [ig], bias=lv[:, 2, kb:kb + 1])
                V = slice(3, 3 + T)
                actf(r_[:, V], r_[:, V], AF.Exp, [r_, small], [r_], scale=cneg[:, kb:kb + 1])
                tt(a2[:, V], r_[:, V], r_[:, V], ALU.mult, [r_], [a2])
                actf(a2[:, V], a2[:, V], AF.Sqrt, [a2], [a2], scale=-1.0, bias=1.0)
                mset(a2[:, 3:4], 1.0, [a2])
                tt(ig[:, V], ig[:, V], a2[:, V], ALU.mult, [ig, a2], [ig])
                tt(ig[:, V], ig[:, V], xl[:, V], ALU.mult, [ig, xl], [ig])
                P.op("dve", lambda e, a2=a2, r_=r_, ig=ig: e.tensor_tensor_scan(out=a2[:, 3:3 + T], data0=r_[:, 3:3 + T], data1=ig[:, 3:3 + T],
                                                                                 initial=0.0, op0=ALU.mult, op1=ALU.add),
                     reads=[r_, ig], writes=[a2])
                actf(xl[:, V], yraw[:, V], AF.Square, [yraw], [xl])
                ts(xl[:, V], xl[:, V], 0.044715, 1.0, ALU.mult, ALU.add, [xl], [xl])
                tt(xl[:, V], xl[:, V], yraw[:, V], ALU.mult, [xl, yraw], [xl])
                actf(xl[:, V], xl[:, V], AF.Sigmoid, [xl], [xl], scale=1.5957691216057308)
                tt(xl[:, V], xl[:, V], yraw[:, V], ALU.mult, [xl, yraw], [xl])
                tt(OB[:], xl[:, V], a2[:, V], ALU.mult, [xl, a2], [OB])
                P.dma("sp", s_oT[1, kb * 128:(kb + 1) * 128, :], OB[:], reads=[OB])

        gkT = Fb[5]; wg = k.sb(st, [16, 512], F32, "wg"); nbg = small[:, 40:44]
        P.dma("pool", wsm[:], wl[:, C_GK:C_GK + 16].rearrange("(kc p) n -> p kc n", p=128), writes=[wsm])
        P.dma("sp", wg[:], gla_w_gate[l], writes=[wg])
        P.dma("sp", nbg, gla_b_gate_t[l], writes=[small])
        ts1(nbg, nbg, -1.0, ALU.mult, [small], [small])
        for tb in range(4):
            p_ = nextp()
            mmB(p_, 16, wsm, 0, hT, tb * 512)
            cpy("act", gkT[0:16, tb * 512:(tb + 1) * 512], p_[0:16, :], [p_], [gkT])
        resetm = Fb[4]
        mset(resetm[:, 0:T], 1.0, [resetm])
        mset(resetm[:, 0:T].rearrange("p (c i) -> p c i", i=128)[:, :, 0:1], 0.0, [resetm])
        Wq = nextw(); wload(Wq, wl[:, C_QC:C_QC + 512], 16, 512)
        Wk = nextw(); wload(Wk, wl[:, C_KC:C_KC + 512], 16, 512)
        for h in range(4):
            sp_, bp, eb, enb = Fb[0], Fb[1], Fb[2], Fb[3]
            for tb in range(4):
                p_ = nextp()
                P.op("pe", lambda e, p_=p_, tb=tb, h=h: e.matmul(p_[:, :], lhsT=wg[0:16, h * 128:(h + 1) * 128], rhs=gkT[0:16, tb * 512:(tb + 1) * 512],
                                                                 start=True, stop=True), reads=[wg, gkT], writes=[p_])
                actf(sp_[:, tb * 512:(tb + 1) * 512], p_[:, :], AF.Exp, [p_, small], [sp_], scale=-1.0, bias=nbg[:, h:h + 1])
            actf(sp_[:, 0:T], sp_[:, 0:T], AF.Ln, [sp_], [sp_], bias=1.0)
            fill(2)
            P.op("dve", lambda e, bp=bp, sp_=sp_: e.tensor_tensor_scan(out=bp[:, 0:T], data0=resetm[:, 0:T], data1=sp_[:, 0:T], initial=0.0,
                                                                       op0=ALU.mult, op1=ALU.add), reads=[resetm, sp_], writes=[bp])
            actf(eb[:, 0:T], bp[:, 0:T], AF.Exp, [bp], [eb], scale=-1.0 / 16)
            actf(enb[:, 0:T], bp[:, 0:T], AF.Exp, [bp], [enb], scale=1.0 / 16)
            bp3 = bp[:, 0:T].rearrange("p (c i) -> p c i", i=128)
            tt(sp_[:, 0:T].rearrange("p (c i) -> p c i", i=128), bp3, bp3[:, :, 127:128].to_broadcast([128, NT, 128]), ALU.subtract, [bp], [sp_])
            actf(sp_[:, 0:T], sp_[:, 0:T], AF.Exp, [sp_], [sp_], scale=1.0 / 16)
            cpy("dve", dcS[:, h, :], eb[:, 0:T].rearrange("p (c i) -> p c i", i=128)[:, :, 127], [eb], [dcS])
            QB, KI, KO = Hb[0], Hb[1], Hb[2]
            for tb in range(4):
                sl = slice(tb * 512, (tb + 1) * 512)
                p_ = nextp()
                mmB(p_, 128, Wq, h * 128, hT, tb * 512)
                stt(QB[:, sl], p_[:, :], 128.0 ** -0.5, eb[:, sl], ALU.mult, ALU.mult, [p_, eb], [QB])
                p2 = nextp()
                mmB(p2, 128, Wk, h * 128, hT, tb * 512)
                tt(KI[:, sl], p2[:, :], enb[:, sl], ALU.mult, [p2, enb], [KI])
                tt(KO[:, sl], p2[:, :], sp_[:, sl], ALU.mult, [p2, sp_], [KO])
            P.dma("sp", s_gq[h * 128:(h + 1) * 128, :], QB[:], reads=[QB])
            P.dma("sp", s_gki[h * 128:(h + 1) * 128, :], KI[:], reads=[KI])
            P.dma("sp", s_gko[h * 128:(h + 1) * 128, :], KO[:], reads=[KO])

        fill(48)

    def head_rs(o2, ss, rs, nh, inv_n):
        P.op("dve", lambda e: e.tensor_reduce(out=ss[:, 0:nh], in_=o2, axis=AX.X, op=ALU.add), reads=[o2b[0]], writes=[ss])
        ts(rs[:, 0:nh], ss[:, 0:nh], inv_n, EPS, ALU.mult, ALU.add, [ss], [rs])
        actf(rs[:, 0:nh], rs[:, 0:nh], AF.Sqrt, [rs], [rs])
        P.op("dve", lambda e: e.reciprocal(out=rs[:, 0:nh], in_=rs[:, 0:nh]), reads=[rs], writes=[rs])
    o2b = [None]

    def phase_gdn(l):
        st = k.phase_begin()
        H = 4
        W_ = H * 128

        def V3(ap):
            return ap.rearrange("p (h i) -> p h i", h=H)
        qB = [k.sb(st, [128, 8, 512], BF16, "qB%d" % i) for i in range(2)]
        kB = [k.sb(st, [128, 8, 512], BF16, "kB%d" % i) for i in range(2)]
        vB = [k.sb(st, [128, 8, 512], BF16, "vB%d" % i) for i in range(2)]
        zt = [k.sb(st, [128, 1024], BF16, "zt%d" % i) for i in range(2)]
        gnr = k.sb(st, [128, 128], F32, "gnr")
        zgb = [k.sb(st, [128, 1024], F32, "zg%d" % i) for i in range(2)]
        P.dma("sp", gnr[:], gdn_norm[l, :].partition_broadcast(128), writes=[gnr])

        def bc_i(ap8):
            return ap8.unsqueeze(2).to_broadcast([128, H, 128])

        def bc_h(ap128):
            return ap128.unsqueeze(1).to_broadcast([128, H, 128])

        class Grp:
            pass
        groups = []
        for g in range(2):
            G = Grp(); G.g = g
            G.f = {n: k.sb(st, [128, W_], F32, "%s%d" % (n, g)) for n in ("GU", "Dm", "DU", "DL", "gamT", "gbL", "egr", "u", "S", "o2", "on")}
            G.b = {n: k.sb(st, [128, W_], BF16, "%s%d" % (n, g)) for n in ("kgb", "ktail", "vb", "Bn", "Aqk", "qd", "Xa", "Xb", "Ya", "Yb", "Pm", "wT", "vn", "Sbf", "oa", "oTs")}
            G.sm = k.sb(st, [128, 64], F32, "sm%d" % g)
            G.ssv = k.sb(st, [128, 8], F32, "ssv%d" % g); G.rsv = k.sb(st, [128, 8], F32, "rsv%d" % g)
            G.ps = [k.ps(st, [128, 512], F32, "psb%d_%d" % (g, i)) for i in range(4)]
            G.np = 0
            mset(G.f["S"][:], 0.0, [G.f["S"]]); mset(G.b["Sbf"][:], 0.0, [G.b["Sbf"]])
            groups.append(G)

        def tile_gen(t, G, q_, k_, v_, zg):
            g = G.g; h0 = g * H
            f32b, bfb, sm = G.f, G.b, G.sm

            def nextp():
                G.np += 1
                return G.ps[G.np % 4]
            tin = t % 4
            cs = slice(tin * 128, (tin + 1) * 128)
            g_t = gS[:, t, h0:h0 + H]; b_t = betaS[:, t, h0:h0 + H]
            S, Sbf = f32b["S"], bfb["Sbf"]
            gcc = sm[:, 0:8]; egc = sm[:, 8:12]; cdec = sm[:, 12:16]; dcol = sm[:, 16:20]; fk = sm[:, 20:24]; nbeta = sm[:, 24:28]
            GU, Dm, DU, DL, gamT, gbL, egr, u_ = (f32b[n] for n in ("GU", "Dm", "DU", "DL", "gamT", "gbL", "egr", "u"))
            tt(V3(GU[:]), bc_h(CF(1)), bc_i(g_t), ALU.mult, [cF, gS], [GU])
            pg = nextp()
            P.op("pe", lambda e: e.matmul(pg[:, :], lhsT=CF(11), rhs=GU[:], start=True, stop=True), reads=[cF, GU], writes=[pg])
            psm = nextp()

            def smm(e):
                e.matmul(psm[:, 0:H], lhsT=CF(1), rhs=g_t, start=True, stop=True)
                return e.matmul(psm[:, H:2 * H], lhsT=CF(11), rhs=g_t, start=True, stop=True)
            P.op("pe", smm, reads=[cF, gS], writes=[psm])
            cpy("dve", gcc, psm[:, 0:2 * H], [psm], [sm])
            actf(egc, sm[:, 0:H], AF.Exp, [sm], [sm])
            actf(cdec, sm[:, H:2 * H], AF.Exp, [sm], [sm])
            tt(dcol, sm[:, H:2 * H], sm[:, 0:H], ALU.subtract, [sm], [sm])
            actf(dcol, dcol, AF.Exp, [sm], [sm])
            tt(fk, egc, b_t, ALU.mult, [sm, betaS], [sm])
            ts1(nbeta, b_t, -1.0, ALU.mult, [betaS], [sm])
            tt(V3(Dm[:]), V3(pg[:, :]), bc_i(sm[:, 0:H]), ALU.subtract, [pg, sm], [Dm])
            actf(egr[:], pg[:, :], AF.Exp, [pg], [egr])
            yield
            tt(V3(DU[:]), V3(Dm[:]), bc_h(CF(2)), ALU.add, [Dm, cF], [DU])
            actf(gamT[:], DU[:], AF.Exp, [DU], [gamT])
            tt(V3(DL[:]), V3(Dm[:]), bc_h(CF(3)), ALU.add, [Dm, cF], [DL])
            actf(DL[:], DL[:], AF.Exp, [DL], [DL], scale=-1.0)
            tt(V3(gbL[:]), V3(DL[:]), bc_i(nbeta), ALU.mult, [DL, sm], [gbL])
            pk = nextp(); pkv = pk[:].bitcast(BF16)[:, 0:W_]

            def trk(e):
                for h in range(H):
                    ins = e.transpose(out=pkv[:, h * 128:(h + 1) * 128], in_=k_[:, h0 + h, cs], identity=CB(0))
                return ins
            P.op("pe", trk, reads=[k_, cB], writes=[pk])
            tt(V3(bfb["kgb"][:]), V3(pkv), bc_i(fk), ALU.mult, [pk, sm], [bfb["kgb"]])
            tt(V3(bfb["ktail"][:]), V3(pkv), bc_i(dcol), ALU.mult, [pk, sm], [bfb["ktail"]])
            pv = nextp(); pvv = pv[:].bitcast(BF16)[:, 0:W_]

            def trv(e):
                for h in range(H):
                    ins = e.transpose(out=pvv[:, h * 128:(h + 1) * 128], in_=v_[:, h0 + h, cs], identity=CB(0))
                return ins
            P.op("pe", trv, reads=[v_, cB], writes=[pv])
            tt(V3(bfb["vb"][:]), V3(pvv), bc_i(b_t), ALU.mult, [pv, betaS], [bfb["vb"]])
            yield
            pG = nextp()

            def gmm(e):
                for h in range(H):
                    ins = e.matmul(pG[:, h * 128:(h + 1) * 128], lhsT=k_[:, h0 + h, cs], rhs=k_[:, h0 + h, cs], start=True, stop=True)
                return ins
            P.op("pe", gmm, reads=[k_], writes=[pG])
            Bn = bfb["Bn"]
            tt(Bn[:], pG[:, :], gbL[:], ALU.mult, [pG, gbL], [Bn])
            pA = nextp()

            def amm(e):
                for h in range(H):
                    ins = e.matmul(pA[:, h * 128:(h + 1) * 128], lhsT=k_[:, h0 + h, cs], rhs=q_[:, h0 + h, cs], start=True, stop=True)
                return ins
            P.op("pe", amm, reads=[k_, q_], writes=[pA])
            Aqk = bfb["Aqk"]
            tt(Aqk[:], pA[:, :], gamT[:], ALU.mult, [pA, gamT], [Aqk])
            qd = bfb["qd"]
            tt(V3(qd[:]), q_[:, h0:h0 + H, cs], V3(egr[:]), ALU.mult, [q_, egr], [qd])
            yield
            X, Xo, Y, Yo, Pm = bfb["Xa"], bfb["Xb"], bfb["Ya"], bfb["Yb"], bfb["Pm"]
            tt(V3(X[:]), V3(Bn[:]), bc_h(CB(4)), ALU.mult, [Bn, cB], [X])
            tt(V3(X[:]), V3(X[:]), bc_h(CB(0)), ALU.add, [X, cB], [X])
            py = nextp(); pyv = py[:].bitcast(BF16)[:, 0:W_]

            def try_(e, X=X):
                for h in range(H):
                    ins = e.transpose(out=pyv[:, h * 128:(h + 1) * 128], in_=X[:, h * 128:(h + 1) * 128], identity=CB(0))
                return ins
            P.op("pe", try_, reads=[X, cB], writes=[py])
            cpy("act", Y[:], pyv, [py], [Y])
            yield
            for lev in range(2, 8):
                pP = nextp()

                def pmm(e, pP=pP, Y=Y):
                    for h in range(H):
                        hs = slice(h * 128, (h + 1) * 128)
                        ins = e.matmul(pP[:, hs], lhsT=Bn[:, hs], rhs=Y[:, hs], start=True, stop=True)
                    return ins
                P.op("pe", pmm, reads=[Bn, Y], writes=[pP])
                tt(V3(Pm[:]), V3(pP[:, :]), bc_h(CF(3 + lev)), ALU.mult, [pP, cF], [Pm])
                yield
                pYn = nextp()

                def ymm(e, pYn=pYn, X=X, Y=Y):
                    for h in range(H):
                        hs = slice(h * 128, (h + 1) * 128)
                        e.matmul(pYn[:, hs], lhsT=CB(0), rhs=Y[:, hs], start=True, stop=False)
                        ins = e.matmul(pYn[:, hs], lhsT=X[:, hs], rhs=Pm[:, hs], start=False, stop=True)
                    return ins
                P.op("pe", ymm, reads=[X, Pm, Y, cB], writes=[pYn])
                cpy("act", Yo[:], pYn[:, :], [pYn], [Yo])
                if lev < 7:
                    pXn = nextp()

                    def xmm(e, pXn=pXn, X=X):
                        for h in range(H):
                            hs = slice(h * 128, (h + 1) * 128)
                            e.matmul(pXn[:, hs], lhsT=CB(0), rhs=X[:, hs], start=True, stop=False)
                            ins = e.matmul(pXn[:, hs], lhsT=Pm[:, hs], rhs=X[:, hs], start=False, stop=True)
                        return ins
                    P.op("pe", xmm, reads=[X, Pm, cB], writes=[pXn])
                    cpy("act", Xo[:], pXn[:, :], [pXn], [Xo])
                    X, Xo = Xo, X
                Y, Yo = Yo, Y
                yield
            pw = nextp()

            def wmm(e, Y=Y):
                for h in range(H):
                    hs = slice(h * 128, (h + 1) * 128)
                    ins = e.matmul(pw[:, hs], lhsT=bfb["kgb"][:, hs], rhs=Y[:, hs], start=True, stop=True)
                return ins
            P.op("pe", wmm, reads=[Y, bfb["kgb"]], writes=[pw])
            wT = bfb["wT"]
            P.op("act", lambda e: e.mul(out=wT[:], in_=pw[:, :], mul=-1.0), reads=[pw], writes=[wT])
            yield
            pws = nextp()

            def wsmm(e, Y=Y):
                for h in range(H):
                    hs = slice(h * 128, (h + 1) * 128)
                    e.matmul(pws[:, hs], lhsT=Y[:, hs], rhs=bfb["vb"][:, hs], start=True, stop=False)
                    ins = e.matmul(pws[:, hs], lhsT=wT[:, hs], rhs=Sbf[:, hs], start=False, stop=True)
                return ins
            P.op("pe", wsmm, reads=[wT, Sbf, Y, bfb["vb"]], writes=[pws])
            vn = bfb["vn"]
            cpy("act", vn[:], pws[:, :], [pws], [vn])
            yield
            po = nextp()

            def omm(e):
                for h in range(H):
                    hs = slice(h * 128, (h + 1) * 128)
                    e.matmul(po[:, hs], lhsT=qd[:, hs], rhs=Sbf[:, hs], start=True, stop=False)
                    ins = e.matmul(po[:, hs], lhsT=Aqk[:, hs], rhs=vn[:, hs], start=False, stop=True)
                return ins
            P.op("pe", omm, reads=[qd, Sbf, Aqk, vn], writes=[po])
            pkv2 = nextp()

            def kvmm(e):
                for h in range(H):
                    hs = slice(h * 128, (h + 1) * 128)
                    ins = e.matmul(pkv2[:, hs], lhsT=bfb["ktail"][:, hs], rhs=vn[:, hs], start=True, stop=True)
                return ins
            P.op("pe", kvmm, reads=[bfb["ktail"], vn], writes=[pkv2])
            tt(V3(S[:]), V3(S[:]), bc_i(cdec), ALU.mult, [S, sm], [S])
            tt(S[:], S[:], pkv2[:, :], ALU.add, [S, pkv2], [S])
            cpy("act", Sbf[:], S[:], [S], [Sbf])
            o2, on, oa, oTs = f32b["o2"], f32b["on"], bfb["oa"], bfb["oTs"]
            actf(o2[:], po[:, :], AF.Square, [po], [o2])
            o2b[0] = o2
            head_rs(V3(o2[:]), G.ssv, G.rsv, H, 1.0 / 128)
            for h in range(H):
                hs = slice(h * 128, (h + 1) * 128)
                actf(on[:, hs], po[:, hs], AF.Identity, [po, G.rsv], [on], scale=G.rsv[:, h:h + 1])
            tt(oa[:], on[:], zg[:, h0 * 128:(h0 + H) * 128], ALU.mult, [on, zg], [oa])
            yield
            pot = nextp(); potv = pot[:].bitcast(BF16)[:, 0:W_]

            def tro(e):
                for h in range(H):
                    hs = slice(h * 128, (h + 1) * 128)
                    ins = e.transpose(out=potv[:, hs], in_=oa[:, hs], identity=CB(0))
                return ins
            P.op("pe", tro, reads=[oa, cB], writes=[pot])
            cpy("act", oTs[:], potv, [pot], [oTs])
            P.dma("sp", s_oT[0].rearrange("(h p) t -> p h t", p=128)[:, h0:h0 + H, t * 128:(t + 1) * 128], V3(oTs[:]), reads=[oTs])

        for t in range(NT):
            blk, tin = t // 4, t % 4
            q_, k_, v_ = qB[blk % 2], kB[blk % 2], vB[blk % 2]
            if tin == 0:
                for dstb, src in ((q_, s_qT), (k_, s_kT), (v_, s_vT)):
                    P.dma("sp", dstb[:], src.rearrange("(h p) t -> p h t", p=128)[:, :, blk * 512:(blk + 1) * 512], writes=[dstb])
            z_ = zt[t % 2]
            P.dma("sp", z_[:], s_za[t * 128:(t + 1) * 128, :], writes=[z_])
            zg = zgb[t % 2]
            P.op("pool", lambda e, zg=zg, z_=z_: e.tensor_tensor(out=zg[:].rearrange("p (h i) -> p h i", h=8), in0=z_[:].rearrange("p (h i) -> p h i", h=8),
                                                               in1=gnr[:].unsqueeze(1).to_broadcast([128, 8, 128]), op=ALU.mult), reads=[z_, gnr], writes=[zg])
            alive = [tile_gen(t, G, q_, k_, v_, zg) for G in groups]
            while alive:
                for gn in list(alive):
                    try:
                        next(gn)
                    except StopIteration:
                        alive.remove(gn)
        k.phase_end(st)

    def phase_gla(l, lmod=None):
        st = k.phase_begin()
        H = 4
        filler = mod_gen(lmod, st, 2) if lmod is not None else None
        qB = [k.sb(st, [128, H, 512], BF16, "gqB%d" % i) for i in range(2)]
        kiB = [k.sb(st, [128, H, 512], BF16, "gkiB%d" % i) for i in range(2)]
        koB = [k.sb(st, [128, H, 512], BF16, "gkoB%d" % i) for i in range(2)]
        vt = [k.sb(st, [128, 1024], BF16, "gvt%d" % i) for i in range(2)]
        zt = [k.sb(st, [128, 1024], BF16, "gzt%d" % i) for i in range(2)]
        S = k.sb(st, [128, 1024], F32, "gS_"); Sbf = k.sb(st, [128, 1024], BF16, "gSbf")
        o2 = k.sb(st, [128, 1024], F32, "go2"); on = k.sb(st, [128, 1024], F32, "gon")
        kot = k.sb(st, [128, 512], BF16, "kot"); AT = k.sb(st, [128, 512], BF16, "gAT")
        oc = k.sb(st, [128, 1024], BF16, "goc"); oTs = k.sb(st, [128, 1024], BF16, "goTs")
        gnr = k.sb(st, [128, 256], F32, "ggnr")
        ssv = k.sb(st, [128, 8], F32, "gssv"); rsv = k.sb(st, [128, 8], F32, "grsv")
        psb = [k.ps(st, [128, 1024], F32, "gps%d" % i) for i in range(3)]
        nP = [0]

        def nextp():
            nP[0] += 1
            return psb[nP[0] % 3]
        P.dma("sp", gnr[:], gla_norm[l, :].partition_broadcast(128), writes=[gnr])
        mset(S[:], 0.0, [S]); mset(Sbf[:], 0.0, [Sbf])
        o2b[0] = o2

        def V4(ap):
            return ap.rearrange("p (h i) -> p h i", h=H)
        for t in range(NT):
            blk, tin = t // 4, t % 4
            q_, ki_, ko_ = qB[blk % 2], kiB[blk % 2], koB[blk % 2]
            if tin == 0:
                for dstb, src in ((q_, s_gq), (ki_, s_gki), (ko_, s_gko)):
                    P.dma("sp", dstb[:], src.rearrange("(h p) t -> p h t", p=128)[:, :, blk * 512:(blk + 1) * 512], writes=[dstb])
            v_, z_ = vt[t % 2], zt[t % 2]
            P.dma("sp", v_[:], s_gv[t * 128:(t + 1) * 128, :], writes=[v_])
            P.dma("sp", z_[:], s_gz[t * 128:(t + 1) * 128, :], writes=[z_])
            cs = slice(tin * 128, (tin + 1) * 128)
            pk = nextp(); pkv = pk[:].bitcast(BF16)[:, 0:512]

            def trk(e, pkv=pkv, ko_=ko_, cs=cs):
                for h in range(H):
                    ins = e.transpose(out=pkv[:, h * 128:(h + 1) * 128], in_=ko_[:, h, cs], identity=CB(0))
                return ins
            P.op("pe", trk, reads=[ko_, cB], writes=[pk])
            cpy("act", kot[:], pkv, [pk], [kot])
            pA = nextp()

            def amm(e, pA=pA, ki_=ki_, q_=q_, cs=cs):
                for h in range(H):
                    ins = e.matmul(pA[:, h * 128:(h + 1) * 128], lhsT=ki_[:, h, cs], rhs=q_[:, h, cs], start=True, stop=True)
                return ins
            P.op("pe", amm, reads=[ki_, q_], writes=[pA])
            tt(V4(AT[:]), V4(pA[:, 0:512]), CF(1).unsqueeze(1).to_broadcast([128, H, 128]), ALU.mult, [pA, cF], [AT])
            po = nextp()

            def omm(e, po=po, q_=q_, v_=v_, cs=cs):
                for h in range(H):
                    e.matmul(po[:, h * 256:(h + 1) * 256], lhsT=q_[:, h, cs], rhs=Sbf[:, h * 256:(h + 1) * 256], start=True, stop=False)
                    ins = e.matmul(po[:, h * 256:(h + 1) * 256], lhsT=AT[:, h * 128:(h + 1) * 128], rhs=v_[:, h * 256:(h + 1) * 256], start=False, stop=True)
                return ins
            P.op("pe", omm, reads=[q_, Sbf, AT, v_], writes=[po])
            pkv2 = nextp()

            def kvmm(e, pkv2=pkv2, v_=v_):
                for h in range(H):
                    ins = e.matmul(pkv2[:, h * 256:(h + 1) * 256], lhsT=kot[:, h * 128:(h + 1) * 128], rhs=v_[:, h * 256:(h + 1) * 256], start=True, stop=True)
                return ins
            P.op("pe", kvmm, reads=[kot, v_], writes=[pkv2])
            S4 = S[:].rearrange("p (h i) -> p h i", h=H)
            tt(S4, S4, dcS[:, :, t].unsqueeze(2).to_broadcast([128, H, 256]), ALU.mult, [S, dcS], [S])
            tt(S[:], S[:], pkv2[:, :], ALU.add, [S, pkv2], [S])
            cpy("act", Sbf[:], S[:], [S], [Sbf])
            actf(o2[:], po[:, :], AF.Square, [po], [o2])
            head_rs(V4(o2[:]), ssv, rsv, 4, 1.0 / 256)
            tt(V4(on[:]), V4(po[:, :]), rsv[:, 0:4].unsqueeze(2).to_broadcast([128, H, 256]), ALU.mult, [po, rsv], [on])
            tt(V4(on[:]), V4(on[:]), gnr[:].unsqueeze(1).to_broadcast([128, H, 256]), ALU.mult, [on, gnr], [on])
            tt(oc[:], on[:], z_[:], ALU.mult, [on, z_], [oc])
            pot = nextp(); potv = pot[:].bitcast(BF16)[:, 0:1024]

            def tro(e, potv=potv):
                for m in range(8):
                    ms = slice(m * 128, (m + 1) * 128)
                    ins = e.transpose(out=potv[:, ms], in_=oc[:, ms], identity=CB(0))
                return ins
            P.op("pe", tro, reads=[oc, cB], writes=[pot])
            cpy("act", oTs[:], potv, [pot], [oTs])
            P.dma("sp", s_oT[2].rearrange("(m p) t -> p m t", p=128)[:, :, t * 128:(t + 1) * 128],
                  oTs[:].rearrange("p (m i) -> p m i", m=8), reads=[oTs])
            if filler is not None:
                for _ in range(2):
                    try:
                        next(filler)
                    except StopIteration:
                        filler = None
                        break
        if filler is not None:
            for _ in filler:
                pass
        k.phase_end(st)

    class WT:
        def __init__(self, halves):
            self.h = halves

        def ap(self, kc, cols):
            return self.h[kc // 8][:, kc % 8, cols]

    def wt_load(pool, nW, src2d, kcs, scr2d=None, first=True, wscr=None):
        hs = []
        for i in range(kcs // 8):
            b = pool[nW[0] % len(pool)]; nW[0] += 1
            rs_ = slice(i * 1024, (i + 1) * 1024)
            if first or scr2d is None:
                P.dma("pool", b[:], src2d[rs_, :].rearrange("(kc p) n -> p kc n", p=128), writes=[b])
                if scr2d is not None:
                    tok = P.dma("sp", scr2d[rs_, :].rearrange("(kc p) n -> p kc n", p=128), b[:], reads=[b])
                    _merge(wscr.lastw, tok)
            else:
                P.dma("sp", b[:], scr2d[rs_, :].rearrange("(kc p) n -> p kc n", p=128), reads=[wscr], writes=[b])
            hs.append(b)
        return WT(hs)

    def load_col(st, l, i, name):
        c = k.sb(st, [128, 16], F32, name)
        P.dma("sp", c[:], rows[l, i, :].rearrange("(kc p) -> p kc", p=128), writes=[c], allow_slow_non_contiguous=True)
        return c

    def norm_A(x1, junk, hn, ss, rs, nev):
        actf(junk, x1, AF.Square, nev[2], [ss_b(nev)] + nev[1], accum_out=ss)
        ts(rs, ss, 1.0 / D, EPS, ALU.mult, ALU.add, [ss_b(nev)], [rs_b(nev)])
        actf(rs, rs, AF.Sqrt, [rs_b(nev)], [rs_b(nev)])
        P.op("dve", lambda e: e.reciprocal(out=rs, in_=rs), reads=[rs_b(nev)], writes=[rs_b(nev)])
        actf(hn, x1, AF.Identity, nev[2] + [rs_b(nev)], nev[3], scale=rs)

    def ss_b(nev):
        return nev[4]

    def rs_b(nev):
        return nev[5]

    def norm_B(hn, hnb, Ac, Bc, hT, tcol, pT, nev):
        for g4 in range(4):
            p_ = pT[nev[0] % len(pT)]

            def tr(e, g4=g4, p_=p_):
                for j in range(4):
                    kc = g4 * 4 + j
                    ins = e.transpose(out=p_[:, j * 128:(j + 1) * 128], in_=hn[:, kc * 128:(kc + 1) * 128], identity=CF(0))
                return ins
            P.op("pe", tr, reads=[hnb, cF], writes=[p_])
            for j in range(4):
                kc = g4 * 4 + j
                dst = hT[:, kc, tcol:tcol + 128]
                src = p_[:, j * 128:(j + 1) * 128]
                if (kc + nev[0]) % 2 == 0:
                    actf(dst, src, AF.Identity, [p_, Ac, Bc], [hT], scale=Ac[:, kc:kc + 1], bias=Bc[:, kc:kc + 1])
                else:
                    ts(dst, src, Ac[:, kc:kc + 1], Bc[:, kc:kc + 1], ALU.mult, ALU.add, [p_, Ac, Bc], [hT])
            nev[0] += 1

    def phase_norm1(l, xsrc, st, hT):
        A1 = load_col(st, l, 0, "A1c"); B1 = load_col(st, l, 1, "B1c")
        xt = [k.sb(st, [128, D], F32, "xt%d" % i) for i in range(2)]
        junk = k.sb(st, [128, D], BF16, "junk")
        hn = [k.sb(st, [128, D], F32, "hn%d" % i) for i in range(2)]
        ss = [k.sb(st, [128, 1], F32, "ss%d" % i) for i in range(2)]
        rs = [k.sb(st, [128, 1], F32, "rs%d" % i) for i in range(2)]
        pT = [k.ps(st, [128, 512], F32, "pT%d" % i) for i in range(6)]
        cnt = [0]

        def A(t):
            x_ = xt[t % 2]
            P.dma("sp", x_[:], xsrc[t * 128:(t + 1) * 128, :], writes=[x_])
            nev = [0, [junk], [x_], [hn[t % 2]], ss[t % 2], rs[t % 2]]
            norm_A(x_[:], junk[:], hn[t % 2][:], ss[t % 2][:], rs[t % 2][:], nev)

        def B(t):
            nev = [cnt[0]]
            norm_B(hn[t % 2], hn[t % 2], A1, B1, hT, t * 128, pT, nev)
            cnt[0] = nev[0]
        A(0)
        for t in range(NT):
            if t + 1 < NT:
                A(t + 1)
            B(t)

    def phase_post(l, xsrc, xdst):
        st = k.phase_begin()
        big3 = k.sb(st, [128, 4, D], F32, "big3")
        U16 = k.sb(st, [128, 16, 512], BF16, "U16")
        oTb = k.sb(st, [128, 24, 512], BF16, "oTb")
        gtb = [k.sb(st, [128, 3, 512], BF16, "gt%d" % i) for i in range(2)]
        mf = [k.sb(st, [128, 512], F32, "mf%d" % i) for i in range(2)]
        y4 = [k.sb(st, [128, D], F32, "y4_%d" % i) for i in range(4)]
        xtb = k.sb(st, [128, D], F32, "xtb")
        rA = k.sb(st, [128, D], F32, "rA")
        h2T = k.sb(st, [128, 16, 512], BF16, "h2T")
        wh = [k.sb(st, [128, 8, 512], BF16, "wh%d" % i) for i in range(5)]
        rG2 = k.sb(st, [128, D], F32, "rG2")
        ss = k.sb(st, [128, 1], F32, "ssp"); rs = k.sb(st, [128, 1], F32, "rsp"); ssr = k.sb(st, [128, 24], F32, "ssr")
        A2 = load_col(st, l, 3, "A2c"); B2 = load_col(st, l, 4, "B2c")
        pp = [k.ps(st, [128, 512], F32, "ppp%d" % i) for i in range(4)]
        pacc = [k.ps(st, [128, 512], F32, "pacc%d" % i) for i in range(4)]
        nW = [0]; nP = [0]; nev = [0, [U16]]
        junk = U16[:].rearrange("p a b -> p (a b)")[:, 0:D]

        def nextp():
            nP[0] += 1
            return pp[nP[0] % 4]

        def load_oT(tb):
            for n in range(3):
                P.dma("sp", oTb[:, n * 8:(n + 1) * 8, :], s_oT[n].rearrange("(kc p) t -> p kc t", p=128)[:, :, tb * 512:(tb + 1) * 512], writes=[oTb])
        wscr = Buf(None)
        load_oT(0)
        P.dma("sp", rA[:], rows[l, 2, :].partition_broadcast(128), writes=[rA])
        P.dma("sp", rG2[:], rows[l, 5, :].partition_broadcast(128), writes=[rG2])
        junk2 = h2T[:].rearrange("p a b -> p (a b)")[:, 0:D]
        m4 = [big3[:, ti, :] for ti in range(4)]

        def f6_stats():
            for ti in range(4):
                actf(junk2, m4[ti], AF.Square, [big3], [h2T, ssr], accum_out=ssr[:, ti:ti + 1])
            ts(ssr[:, 8:12], ssr[:, 0:4], 1.0 / D, EPS, ALU.mult, ALU.add, [ssr], [ssr])
            actf(ssr[:, 8:12], ssr[:, 8:12], AF.Sqrt, [ssr], [ssr])
            P.op("dve", lambda e: e.reciprocal(out=ssr[:, 8:12], in_=ssr[:, 8:12]), reads=[ssr], writes=[ssr])
            ts(ssr[:, 16:20], ssr[:, 8:12], 1.0, None, ALU.mult, ALU.bypass, [ssr], [ssr]) if False else cpy("dve", ssr[:, 16:20], ssr[:, 8:12], [ssr], [ssr])

        def f6_tile(tbp, ti):
            r0 = tbp * 512 + ti * 128
            stt(m4[ti], m4[ti], ssr[:, 16 + ti:17 + ti], rG2[:], ALU.mult, ALU.mult, [big3, ssr, rG2], [big3])
            tt(y4[ti][:], y4[ti][:], m4[ti], ALU.add, [y4[ti], big3], [y4[ti]])
            P.dma("sp", xdst[r0:r0 + 128, :], y4[ti][:], reads=[y4[ti]])
        for tb in range(4):
            tsl = slice(tb * 512, (tb + 1) * 512)
            for c4 in range(4):
                Wn = [wt_load(wh, nW, w_branch[l, n, :, c4 * 512:(c4 + 1) * 512], 8) for n in range(3)]
                for cg in range(4):
                    cga = c4 * 4 + cg
                    g_ = gtb[cga % 2]
                    P.dma("sp", g_[:], s_gates.rearrange("(n r) t -> r n t", n=3)[cga * 128:(cga + 1) * 128, :, tsl], writes=[g_])
                    m_ = mf[cga % 2]
                    for n in range(3):
                        p_ = nextp()

                        def f(e, p_=p_, W=Wn[n], n=n, cg=cg):
                            for kc in range(8):
                                ins = e.matmul(p_[:, :], lhsT=W.ap(kc, slice(cg * 128, (cg + 1) * 128)), rhs=oTb[:, n * 8 + kc, :], start=(kc == 0), stop=(kc == 7))
                            return ins
                        P.op("pe", f, reads=Wn[n].h + [oTb], writes=[p_])
                        if n == 0:
                            tt(m_[:], p_[:, :], g_[:, 0, :], ALU.mult, [p_, g_], [m_])
                        elif n == 1:
                            tt(xtb[:, 0:512], p_[:, :], g_[:, 1, :], ALU.mult, [p_, g_], [xtb])
                            tt(m_[:], m_[:], xtb[:, 0:512], ALU.add, [m_, xtb], [m_])
                        else:
                            tt(xtb[:, 512:1024], p_[:, :], g_[:, 2, :], ALU.mult, [p_, g_], [xtb])
                            tt(U16[:, cga, :], m_[:], xtb[:, 512:1024], ALU.add, [m_, xtb], [U16])
                if tb > 0:
                    f6_tile(tb - 1, c4)
            if tb < 3:
                load_oT(tb + 1)
            for nch in range(4):
                W = wt_load(wh, nW, w_out[l, :, nch * 512:(nch + 1) * 512], 16)
                for ti in range(4):
                    p_ = nextp()

                    def f(e, p_=p_, W=W, ti=ti):
                        for kc in range(16):
                            ins = e.matmul(p_[:, :], lhsT=U16[:, kc, ti * 128:(ti + 1) * 128], rhs=W.ap(kc, slice(0, 512)), start=(kc == 0), stop=(kc == 15))
                        return ins
                    P.op("pe", f, reads=W.h + [U16], writes=[p_])
                    cpy("act" if ti % 2 else "dve", y4[ti][:, nch * 512:(nch + 1) * 512], p_[:, :], [p_], [y4[ti]])
            for ti in range(4):
                y_ = y4[ti]
                actf(junk, y_[:], AF.Square, [y_], [U16, ssr], accum_out=ssr[:, ti:ti + 1])
            ts(ssr[:, 8:12], ssr[:, 0:4], 1.0 / D, EPS, ALU.mult, ALU.add, [ssr], [ssr])
            actf(ssr[:, 8:12], ssr[:, 8:12], AF.Sqrt, [ssr], [ssr])
            P.op("dve", lambda e: e.reciprocal(out=ssr[:, 8:12], in_=ssr[:, 8:12]), reads=[ssr], writes=[ssr])
            for ti in range(4):
                r0 = tb * 512 + ti * 128
                xb_, xa_ = (xtb, xtb[:]) if ti % 2 == 0 else (big3, big3[:, 0, :])
                P.dma("sp", xa_, xsrc[r0:r0 + 128, :], writes=[xb_])
                y_ = y4[ti]
                stt(y_[:], y_[:], ssr[:, 8 + ti:9 + ti], rA[:], ALU.mult, ALU.mult, [y_, ssr, rA], [y_])
                P.op("dve", lambda e, y_=y_, xa_=xa_: e.tensor_tensor(out=y_[:], in0=y_[:], in1=xa_, op=ALU.add), reads=[y_, xb_], writes=[y_])
            for ti in range(4):
                actf(junk, y4[ti][:], AF.Square, [y4[ti]], [U16, ssr], accum_out=ssr[:, 4 + ti:5 + ti])
            ts(ssr[:, 12:16], ssr[:, 4:8], 1.0 / D, EPS, ALU.mult, ALU.add, [ssr], [ssr])
            actf(ssr[:, 12:16], ssr[:, 12:16], AF.Sqrt, [ssr], [ssr])
            P.op("dve", lambda e: e.reciprocal(out=ssr[:, 12:16], in_=ssr[:, 12:16]), reads=[ssr], writes=[ssr])
            for ti in range(4):
                actf(big3[:, ti, :], y4[ti][:], AF.Identity, [y4[ti], ssr], [big3], scale=ssr[:, 12 + ti:13 + ti])
            for ti in range(4):
                nevb = [nev[0]]
                norm_B(big3[:, ti, :], big3, A2, B2, h2T, ti * 128, pp, nevb)
                nev[0] = nevb[0]
            for qq in range(4):
                for c4 in range(4):
                    c0 = qq * 2048 + c4 * 512
                    W = wt_load(wh, nW, w_mlp1[l, :, c0:c0 + 512], 16)
                    for cg in range(4):
                        p_ = nextp()

                        def f(e, p_=p_, W=W, cg=cg):
                            for kc in range(16):
                                ins = e.matmul(p_[:, :], lhsT=W.ap(kc, slice(cg * 128, (cg + 1) * 128)), rhs=h2T[:, kc, :], start=(kc == 0), stop=(kc == 15))
                            return ins
                        P.op("pe", f, reads=W.h + [h2T], writes=[p_])
                        m_ = mf[cg % 2]
                        actf(m_[:], p_[:, :], AF.Square, [p_], [m_])
                        stt(U16[:, c4 * 4 + cg, :], p_[:, :], 0.0, m_[:], ALU.is_gt, ALU.mult, [p_, m_], [U16])
                for nch in range(4):
                    for kq in range(2):
                        r0 = (qq * 16 + kq * 8) * 128
                        W = wt_load(wh, nW, w_mlp2[l, r0:r0 + 1024, nch * 512:(nch + 1) * 512], 8)

                        def f(e, W=W, kq=kq):
                            for kc in range(8):
                                for ti in range(4):
                                    ins = e.matmul(pacc[ti][:, :], lhsT=U16[:, kq * 8 + kc, ti * 128:(ti + 1) * 128], rhs=W.ap(kc, slice(0, 512)),
                                                   start=(kq == 0 and kc == 0), stop=(kq == 1 and kc == 7))
                            return ins
                        P.op("pe", f, reads=W.h + [U16], writes=pacc)
                    for ti in range(4):
                        dst = m4[ti][:, nch * 512:(nch + 1) * 512]
                        if qq == 0:
                            cpy("act" if ti % 2 else "dve", dst, pacc[ti][:, :], [pacc[ti]], [big3])
                        else:
                            tt(dst, pacc[ti][:, :], dst, ALU.add, [pacc[ti], big3], [big3])
            f6_stats()
            if tb == 3:
                for ti in range(4):
                    f6_tile(tb, ti)
        k.phase_end(st)

    for l in range(nlayers):
        xsrc = x_in if l == 0 else xs0
        xdst = xs0 if (l == 0 and nlayers > 1) else out
        if l == 0:
            phase_mod(l)
        if stop == "mod":
            break
        st = k.phase_begin()
        hT = k.sb(st, [128, 16, T], BF16, "hT")
        st1 = ExitStack()
        live_save = k.live
        k.live = []
        phase_norm1(l, xsrc, st1, hT)
        if s_hdbg is not None and l == 0:
            P.dma("sp", s_hdbg.rearrange("(kc p) t -> p kc t", p=128), hT[:], reads=[hT])
        P.barrier(); P.emit(); P.release(k.live); st1.close()
        k.live = live_save
        if stop == "norm1":
            k.phase_end(st)
            break
        phase_inproj(l, st, hT)
        k.phase_end(st)
        if stop == "inproj":
            break
        phase_gdn(l)
        if stop == "gdn":
            break
        phase_gla(l, (l + 1) if (l + 1 < nlayers) else None)
        if stop == "gla":
            break
        phase_post(l, xsrc, xdst)
        P.fresh_engine_sems()

    P.barrier()
    P.emit()
    pst.close()
    k.es.close()
    return k


def host_inputs(inputs, b):
    f = np.float32
    d = {}
    d["x"] = np.ascontiguousarray(inputs["x"][b], dtype=f)
    d["c_t"] = np.ascontiguousarray(np.asarray(inputs["c"][b], dtype=f).reshape(16, 128).T)
    d["consts"] = _consts()
    for n in ("w_ada", "b_ada", "g_pre_mix", "g_post_mix", "g_pre_mlp", "g_post_mlp", "w_in", "gdn_a_log", "gdn_dt_bias",
              "gdn_norm", "lru_w_a", "lru_w_i", "gla_w_gate", "gla_norm", "w_branch", "w_out", "w_mlp1", "w_mlp2"):
        d[n] = np.ascontiguousarray(inputs[n], dtype=f)
    d["conv_gdn_t"] = np.ascontiguousarray(np.asarray(inputs["conv_gdn"], f).reshape(2, 4, 24, 128).transpose(0, 3, 2, 1))
    d["conv_lru_t"] = np.ascontiguousarray(np.asarray(inputs["conv_lru"], f).reshape(2, 4, 8, 128).transpose(0, 3, 2, 1))
    lv = np.stack([np.asarray(inputs[n], f).reshape(2, 8, 128) for n in ("conv_lru_b", "lru_b_a", "lru_b_i", "lru_lambda")], axis=1)
    d["lru_vec_t"] = np.ascontiguousarray(lv.transpose(0, 3, 1, 2))
    d["gla_b_gate_t"] = np.ascontiguousarray(np.asarray(inputs["gla_b_gate"], f).reshape(2, 4, 128).transpose(0, 2, 1))
    return d


_CACHE = {}


def kernel(**inputs):
    if "k" not in _CACHE:
        _CACHE["k"] = build()
    k = _CACHE["k"]
    in_maps = [host_inputs(inputs, b) for b in range(8)]
    res = run_bass_kernel_spmd(k.nc, in_maps, core_ids=list(range(8)))
    return np.stack([np.asarray(r["out"], dtype=np.float32) for r in res.results], axis=0)
```

```python
import numpy as np
from contextlib import ExitStack
import concourse.bass as bass
import concourse.mybir as mybir
from concourse.bass_utils import run_bass_kernel_spmd

F32 = mybir.dt.float32
BF16 = mybir.dt.bfloat16
AF = mybir.ActivationFunctionType
ALU = mybir.AluOpType
AX = mybir.AxisListType

T = 2048
D = 2048
NT = T // 128
DIN = 15392
DFF = 8192
EPS = 1e-6
C_QA, C_KA, C_VA, C_ZA, C_BETA, C_ALPHA, C_XB, C_YB, C_QC, C_KC, C_VC, C_ZC, C_GK, C_GATES = (
    0, 1024, 2048, 3072, 4096, 4104, 4112, 5136, 6160, 6672, 7184, 8208, 9232, 9248)
BIG = 1.0e30


class Sem:
    __slots__ = ("h", "val")

    def __init__(self, h):
        self.h = h
        self.val = 0


class Buf:
    __slots__ = ("t", "lastw", "readers", "sem_in", "sem_out")

    def __init__(self, t):
        self.t = t
        self.lastw = {}
        self.readers = {}
        self.sem_in = None
        self.sem_out = None

    def __getitem__(self, k):
        return self.t[k]


def _merge(d, tok):
    for s, v in tok.items():
        if d.get(s, 0) < v:
            d[s] = v


class Prog:
    ENGS = ("sp", "act", "dve", "pool", "pe")

    def __init__(self, nc, es):
        self.nc = nc
        self.es = es
        self.free = []
        self.free_k = {}
        self.nsem = 0
        self.ops = {e: [] for e in self.ENGS}
        self.esem = {}
        self.waited = {e: {} for e in self.ENGS}
        self.pending = {}
        self.bar = self.new_sem()
        self.fresh_engine_sems()

    def new_sem(self):
        if self.free:
            return self.free.pop()
        self.nsem += 1
        return Sem(self.es.enter_context(self.nc.semaphore("s%d" % self.nsem)))

    def fresh_engine_sems(self):
        for e in self.ENGS:
            self.esem[e] = self.new_sem()

    def buf(self, t):
        return Buf(t)

    def _deps(self, eng, reads, writes):
        deps = {}
        for b in reads:
            _merge(deps, b.lastw)
        for b in writes:
            _merge(deps, b.lastw)
            _merge(deps, b.readers)
        waits = []
        w = self.waited[eng]
        for s, v in deps.items():
            if w.get(s, 0) < v:
                waits.append((s, v))
                w[s] = v
        return waits

    def _commit(self, tok, reads, writes):
        for b in reads:
            _merge(b.readers, tok)
        for b in writes:
            _merge(b.lastw, tok)
            b.readers = {}

    def op(self, eng, fn, reads=(), writes=()):
        waits = self._deps(eng, reads, writes)
        s = self.esem[eng]
        s.val += 1
        assert s.val < 60000
        tok = {s: s.val}
        self.ops[eng].append((waits, fn, s, 1))
        self._commit(tok, reads, writes)
        return tok

    def dma(self, q, out, in_, reads=(), writes=(), **kw):
        waits = self._deps(q, reads, writes)
        kind = "sw" if q == "pool" else "hw"
        b = writes[0] if writes else reads[0]
        key = ("in" if writes else "out", kind)
        if b.sem_in is None:
            b.sem_in = {}
        if key not in b.sem_in:
            fl = self.free_k.setdefault(kind, [])
            if fl:
                b.sem_in[key] = fl.pop()
            else:
                self.nsem += 1
                b.sem_in[key] = Sem(self.es.enter_context(self.nc.semaphore("s%d" % self.nsem)))
        s = b.sem_in[key]
        s.val += 16
        assert s.val < 60000
        tok = {s: s.val}
        self.ops[q].append((waits, lambda e: e.dma_start(out=out, in_=in_, **kw), s, 16))
        self._commit(tok, reads, writes)
        _merge(self.pending, tok)
        return tok

    def release(self, bufs):
        for b in bufs:
            if b.sem_in:
                for (d, kind), s in b.sem_in.items():
                    self.free_k.setdefault(kind, []).append(s)
            b.sem_in = None

    def barrier(self):
        deps = dict(self.pending)
        for e in self.ENGS:
            s = self.esem[e]
            if s.val:
                deps[s] = max(deps.get(s, 0), s.val)
        w = self.waited["sp"]
        waits = [(s, v) for s, v in deps.items() if w.get(s, 0) < v]
        self.bar.val += 1
        bv = self.bar.val
        bar = self.bar
        self.ops["sp"].append((waits, lambda e: e.sem_inc(bar.h, 1), None, 0))
        for e in self.ENGS:
            if e != "sp":
                self.ops[e].append(([(bar, bv)], None, None, 0))
            for s, v in deps.items():
                self.waited[e][s] = max(self.waited[e].get(s, 0), v)
        self.pending = {}

    def emit(self):
        nc = self.nc
        with nc.Block() as block:
            table = (("sp", block.sync), ("act", block.scalar), ("dve", block.vector),
                     ("pool", block.gpsimd), ("pe", block.tensor))
            for name, deco in table:
                ops = self.ops[name]

                def body(e, ops=ops):
                    for waits, fn, s, inc in ops:
                        for ws, wv in waits:
                            e.wait_ge(ws.h, wv)
                        if fn is None:
                            continue
                        ins = fn(e)
                        if s is not None:
                            ins.then_inc(s.h, inc)
                deco(body)
                self.ops[name] = []


def _consts():
    p = np.arange(128)[:, None]
    f = np.arange(128)[None, :]
    mats = []
    mats.append((p == f).astype(np.float32))
    mats.append((p <= f).astype(np.float32))
    mats.append(np.where(p <= f, 0.0, -BIG).astype(np.float32))
    mats.append(np.where(f < p, 0.0, BIG).astype(np.float32))
    mats.append(((p // 2 == f // 2) & (p % 2 == 1) & (f % 2 == 0)).astype(np.float32))
    for l in range(2, 8):
        s = 2 ** (l - 1)
        mats.append(((p // (2 * s) == f // (2 * s)) & (p % (2 * s) < s) & (f % (2 * s) >= s)).astype(np.float32))
    mats.append(np.ones((128, 128), np.float32))
    return np.concatenate(mats, axis=1)


NCONST = 12


class K:
    def __init__(self, debug=()):
        self.debug = set(debug)
        self.nc = bass.Bass("TRN2", target_bir_lowering=False)
        self.es = ExitStack()
        self.P = Prog(self.nc, self.es)
        self.uid = 0
        self.inp = {}
        self.outs = []

    def din(self, name, shape, dt=F32):
        a = self.nc.dram_tensor(name, list(shape), dt, kind="ExternalInput").ap()
        self.inp[name] = a
        return a

    def dscr(self, name, shape, dt):
        kind = "ExternalOutput" if name in self.debug else "Internal"
        if kind == "ExternalOutput":
            self.outs.append(name)
        return self.nc.dram_tensor(name, list(shape), dt, kind=kind).ap()

    def sb(self, st, shape, dt, name=None):
        self.uid += 1
        t = st.enter_context(self.nc.sbuf_tensor("%s_%d" % (name or "t", self.uid), list(shape), dt))
        b = Buf(t)
        self.live.append(b)
        return b

    def ps(self, st, shape, dt, name=None):
        self.uid += 1
        t = st.enter_context(self.nc.psum_tensor("%s_%d" % (name or "p", self.uid), list(shape), dt))
        b = Buf(t)
        self.live.append(b)
        return b

    def phase_begin(self):
        self.live = []
        return ExitStack()

    def phase_end(self, st):
        self.P.barrier()
        self.P.emit()
        self.P.release(self.live)
        st.close()


def build(debug=(), nlayers=2, stop=None):
    k = K(debug)
    nc, P = k.nc, k.P
    x_in = k.din("x", [T, D])
    c_t = k.din("c_t", [128, 16])
    consts_d = k.din("consts", [128, NCONST * 128])
    w_ada = k.din("w_ada", [2, D, 6 * D]); b_ada = k.din("b_ada", [2, 6 * D])
    g_pre_mix = k.din("g_pre_mix", [2, D]); g_post_mix = k.din("g_post_mix", [2, D])
    g_pre_mlp = k.din("g_pre_mlp", [2, D]); g_post_mlp = k.din("g_post_mlp", [2, D])
    w_in = k.din("w_in", [2, D, DIN])
    conv_gdn_t = k.din("conv_gdn_t", [2, 128, 24, 4])
    gdn_a_log = k.din("gdn_a_log", [2, 8]); gdn_dt_bias = k.din("gdn_dt_bias", [2, 8]); gdn_norm = k.din("gdn_norm", [2, 128])
    conv_lru_t = k.din("conv_lru_t", [2, 128, 8, 4]); lru_vec_t = k.din("lru_vec_t", [2, 128, 4, 8])
    lru_w_a = k.din("lru_w_a", [2, 8, 128, 128]); lru_w_i = k.din("lru_w_i", [2, 8, 128, 128])
    gla_w_gate = k.din("gla_w_gate", [2, 16, 512]); gla_b_gate_t = k.din("gla_b_gate_t", [2, 128, 4]); gla_norm = k.din("gla_norm", [2, 256])
    w_branch = k.din("w_branch", [2, 3, 1024, D]); w_out = k.din("w_out", [2, D, D])
    w_mlp1 = k.din("w_mlp1", [2, D, DFF]); w_mlp2 = k.din("w_mlp2", [2, DFF, D])
    out = nc.dram_tensor("out", [T, D], F32, kind="ExternalOutput").ap()
    k.outs.append("out")
    rows = k.dscr("rows", [2, 6, D], F32)
    xs0 = k.dscr("xs0", [T, D], F32)
    s_qT = k.dscr("s_qT", [1024, T], BF16); s_kT = k.dscr("s_kT", [1024, T], BF16); s_vT = k.dscr("s_vT", [1024, T], BF16)
    s_za = k.dscr("s_za", [T, 1024], BF16)
    s_oT = k.dscr("s_oT", [3, 1024, T], BF16)
    s_gq = k.dscr("s_gq", [512, T], BF16); s_gki = k.dscr("s_gki", [512, T], BF16); s_gko = k.dscr("s_gko", [512, T], BF16)
    s_gv = k.dscr("s_gv", [T, 1024], BF16); s_gz = k.dscr("s_gz", [T, 1024], BF16)
    s_gates = k.dscr("s_gates", [6144, T], BF16)
    s_hdbg = k.dscr("s_hdbg", [D, T], BF16) if "s_hdbg" in k.debug else None

    pst = k.phase_begin()
    cF = k.sb(pst, [128, NCONST * 128], F32, "cF")
    cB = k.sb(pst, [128, NCONST * 128], BF16, "cB")
    betaS = k.sb(pst, [128, NT, 8], F32, "betaS")
    gS = k.sb(pst, [128, NT, 8], F32, "gS")
    dcS = k.sb(pst, [128, 4, NT], F32, "dcS")
    persist = k.live

    def CF(i):
        return cF[:, i * 128:(i + 1) * 128]

    def CB(i):
        return cB[:, i * 128:(i + 1) * 128]

    P.dma("sp", cF[:], consts_d, writes=[cF])
    P.op("dve", lambda e: e.tensor_copy(out=cB[:], in_=cF[:]), reads=[cF], writes=[cB])

    def mod_gen(l, st, nwb):
        csrc = k.sb(st, [128, 16], F32); csil = k.sb(st, [128, 16], BF16)
        vec = [k.sb(st, [1, D], F32, "vec%d" % i) for i in range(6)]
        bad = [k.sb(st, [1, D], F32, "bad%d" % i) for i in range(2)]
        gv = [k.sb(st, [1, D], F32, "gv%d" % i) for i in range(4)]
        wb = [k.sb(st, [128, 16, 512], BF16, "wbm%d" % i) for i in range(nwb)]
        pp = [k.ps(st, [128, 512], F32, "pp%d" % i) for i in range(2)]
        P.dma("sp", csrc[:], c_t, writes=[csrc])
        P.op("act", lambda e: e.activation(out=csil[:], in_=csrc[:], func=AF.Silu), reads=[csrc], writes=[csil])
        for i, gsrc in enumerate((g_pre_mix, g_post_mix, g_pre_mlp, g_post_mlp)):
            P.dma("sp", gv[i][:], gsrc[l:l + 1, :], writes=[gv[i]])
        n = 0
        for m in range(6):
            bd = bad[m % 2]
            P.dma("sp", bd[:], b_ada[l:l + 1, m * D:(m + 1) * D], writes=[bd])
            for nch in range(4):
                w = wb[n % nwb]; p_ = pp[n % 2]; n += 1
                c0 = m * D + nch * 512
                P.dma("pool", w[:], w_ada[l, :, c0:c0 + 512].rearrange("(kc p) n -> p kc n", p=128), writes=[w])

                def mm(e, w=w, p_=p_):
                    for kc in range(16):
                        ins = e.matmul(p_[0:1, :], lhsT=csil[:, kc:kc + 1], rhs=w[:, kc, :], start=(kc == 0), stop=(kc == 15))
                    return ins
                P.op("pe", mm, reads=[w, csil], writes=[p_])
                P.op("dve", lambda e, p_=p_, m=m, nch=nch, bd=bd: e.tensor_tensor(
                    out=vec[m][0:1, nch * 512:(nch + 1) * 512], in0=p_[0:1, :], in1=bd[0:1, nch * 512:(nch + 1) * 512], op=ALU.add),
                    reads=[p_, bd], writes=[vec[m]])
                yield
        sh1, sc1, gt1, sh2, sc2, gt2 = vec
        combos = [(sc1, gv[0], True), (sh1, None, False), (gt1, gv[1], False),
                  (sc2, gv[2], True), (sh2, None, False), (gt2, gv[3], False)]
        for i, (a, g, plus1) in enumerate(combos):
            if g is None:
                pass
            elif plus1:
                P.op("dve", lambda e, a=a, g=g: e.scalar_tensor_tensor(out=a[:], in0=a[:], scalar=1.0, in1=g[:], op0=ALU.add, op1=ALU.mult),
                     reads=[a, g], writes=[a])
            else:
                P.op("dve", lambda e, a=a, g=g: e.tensor_tensor(out=a[:], in0=a[:], in1=g[:], op=ALU.mult), reads=[a, g], writes=[a])
            P.dma("sp", rows[l, i:i + 1, :], a[:], reads=[a])

    def phase_mod(l):
        st = k.phase_begin()
        for _ in mod_gen(l, st, 3):
            pass
        k.phase_end(st)

    def rstd(st_bufs, src, ss, rs, scale_inv_n):
        P.op("dve", lambda e: e.tensor_scalar(out=rs[:], in0=ss[:], scalar1=scale_inv_n, scalar2=EPS, op0=ALU.mult, op1=ALU.add),
             reads=[ss], writes=[rs])
        P.op("act", lambda e: e.activation(out=rs[:], in_=rs[:], func=AF.Sqrt), reads=[rs], writes=[rs])
        P.op("dve", lambda e: e.reciprocal(out=rs[:], in_=rs[:]), reads=[rs], writes=[rs])

    def wload(wb, src2d, kcs, ncols):
        P.dma("pool", wb[:, 0:kcs, 0:ncols], src2d.rearrange("(kc p) n -> p kc n", p=128), writes=[wb])

    def mmB(ps, M, W, c, act, t0, n=512, KC=16):
        def f(e):
            for kc in range(KC):
                ins = e.matmul(ps[0:M, 0:n], lhsT=W[:, kc, c:c + M], rhs=act[:, kc, t0:t0 + n], start=(kc == 0), stop=(kc == KC - 1))
            return ins
        P.op("pe", f, reads=[W, act], writes=[ps])

    def mmA(ps, W, n, act, t0, KC=16):
        def f(e):
            for kc in range(KC):
                ins = e.matmul(ps[:, 0:n], lhsT=act[:, kc, t0:t0 + 128], rhs=W[:, kc, 0:n], start=(kc == 0), stop=(kc == KC - 1))
            return ins
        P.op("pe", f, reads=[W, act], writes=[ps])

    def actf(out, in_, func, reads, writes, **kw):
        P.op("act", lambda e: e.activation(out=out, in_=in_, func=func, **kw), reads=reads, writes=writes)

    def tt(out, in0, in1, op, reads, writes, eng="dve"):
        P.op(eng, lambda e: e.tensor_tensor(out=out, in0=in0, in1=in1, op=op), reads=reads, writes=writes)

    def stt(out, in0, scalar, in1, op0, op1, reads, writes):
        P.op("dve", lambda e: e.scalar_tensor_tensor(out=out, in0=in0, scalar=scalar, in1=in1, op0=op0, op1=op1), reads=reads, writes=writes)

    def ts(out, in0, s1, s2, op0, op1, reads, writes):
        P.op("dve", lambda e: e.tensor_scalar(out=out, in0=in0, scalar1=s1, scalar2=s2, op0=op0, op1=op1), reads=reads, writes=writes)

    def ts1(out, in0, s1, op0, reads, writes):
        P.op("dve", lambda e: e.tensor_scalar(out=out, in0=in0, scalar1=s1, scalar2=None, op0=op0), reads=reads, writes=writes)

    def cpy(eng, out, in_, reads, writes):
        if eng == "act":
            P.op("act", lambda e: e.copy(out=out, in_=in_), reads=reads, writes=writes)
        else:
            P.op("dve", lambda e: e.tensor_copy(out=out, in_=in_), reads=reads, writes=writes)

    def mset(out, val, writes):
        P.op("dve", lambda e: e.memset(out, val), writes=writes)

    def phase_inproj(l, st, hT):
        wb = [k.sb(st, [128, 16, 512], BF16, "wb%d" % i) for i in range(3)]
        wsm = k.sb(st, [128, 16, 16], BF16, "wsm")
        Fb = [k.sb(st, [128, 2051], F32, "F%d" % i) for i in range(6)]
        Hb = [k.sb(st, [128, T], BF16, "H%d" % i) for i in range(4)]
        pp = [k.ps(st, [128, 512], F32, "pp%d" % i) for i in range(6)]
        cw = k.sb(st, [128, 24, 4], F32, "cw"); cwl = k.sb(st, [128, 8, 4], F32, "cwl"); lv = k.sb(st, [128, 4, 8], F32, "lv")
        small = k.sb(st, [128, 64], F32, "small")
        epsT = k.sb(st, [128, 1], F32, "epsT")
        wgi = [k.sb(st, [128, 128], BF16, "wgi%d" % i) for i in range(2)]
        nW = [0]; nP = [0]

        def nextw():
            nW[0] += 1
            return wb[nW[0] % 2]

        def nextp():
            nP[0] += 1
            return pp[nP[0] % 6]
        wl = w_in[l]
        P.dma("sp", cw[:], conv_gdn_t[l], writes=[cw])
        P.dma("sp", cwl[:], conv_lru_t[l], writes=[cwl])
        P.dma("sp", lv[:], lru_vec_t[l], writes=[lv])
        mset(epsT[:], EPS, [epsT])
        for f_ in Fb:
            mset(f_[:, 0:3], 0.0, [f_])

        def proj_rows(dst, W, c, act_eng_toggle=[0]):
            for tb in range(4):
                p_ = nextp()
                mmB(p_, 128, W, c, hT, tb * 512)
                act_eng_toggle[0] += 1
                cpy("act" if act_eng_toggle[0] % 2 else "dve", dst[:, 3 + tb * 512:3 + (tb + 1) * 512], p_[:, :], [p_], [dst])

        def conv4(acc, raw, wv, bias=None):
            if bias is None:
                ts1(acc[:, 3:3 + T], raw[:, 0:T], wv[:, 0:1], ALU.mult, [raw], [acc])
            else:
                ts(acc[:, 3:3 + T], raw[:, 0:T], wv[:, 0:1], bias, ALU.mult, ALU.add, [raw], [acc])
            for j in range(1, 4):
                stt(acc[:, 3:3 + T], raw[:, j:j + T], wv[:, j:j + 1], acc[:, 3:3 + T], ALU.mult, ALU.add, [raw, acc], [acc])

        GBs = [k.sb(st, [128, T], BF16, "GB%d" % i) for i in range(2)]

        def gates_gen():
            for wt in range(12):
                W = wb[2]
                wload(W, wl[:, C_GATES + wt * 512:C_GATES + (wt + 1) * 512], 16, 512)
                for cg in range(4):
                    GB = GBs[(wt * 4 + cg) % 2]
                    for tb in range(4):
                        p_ = nextp()
                        mmB(p_, 128, W, cg * 128, hT, tb * 512)
                        actf(GB[:, tb * 512:(tb + 1) * 512], p_[:, :], AF.Sigmoid, [p_], [GB])
                    r0 = (wt * 4 + cg) * 128
                    P.dma("sp", s_gates[r0:r0 + 128, :], GB[:], reads=[GB])
                    yield
        gg = gates_gen()

        def fill(n):
            for _ in range(n):
                try:
                    next(gg)
                except StopIteration:
                    return

        nh = 0
        for f in range(3):
            for g4 in range(2):
                W = nextw()
                wload(W, wl[:, f * 1024 + g4 * 512: f * 1024 + (g4 + 1) * 512], 16, 512)
                for cg in range(4):
                    head = g4 * 4 + cg
                    cgi = f * 8 + head
                    raw, acc, rsb = Fb[nh % 2], Fb[2 + nh % 2], Fb[4]
                    sil = acc
                    ob = Hb[nh % 2]; sq = Hb[2 + nh % 2]; nh += 1
                    proj_rows(raw, W, cg * 128)
                    fill(1)
                    conv4(acc, raw, cw[:, cgi, :])
                    if f == 2:
                        actf(ob[:], acc[:, 3:3 + T], AF.Silu, [acc], [ob])
                        P.dma("sp", s_vT[head * 128:(head + 1) * 128, :], ob[:], reads=[ob])
                    else:
                        actf(sil[:, 3:3 + T], acc[:, 3:3 + T], AF.Silu, [acc], [sil])
                        actf(sq[:], sil[:, 3:3 + T], AF.Square, [sil], [sq])
                        for tb in range(4):
                            p_ = nextp()
                            P.op("pe", lambda e, p_=p_, sq=sq, tb=tb: e.matmul(p_[:, :], lhsT=CB(11), rhs=sq[:, tb * 512:(tb + 1) * 512], start=True, stop=True),
                                 reads=[sq, cB], writes=[p_])
                            actf(rsb[:, 3 + tb * 512:3 + (tb + 1) * 512], p_[:, :], AF.Ln, [p_, epsT], [rsb], bias=epsT[:, 0:1])
                        actf(rsb[:, 3:3 + T], rsb[:, 3:3 + T], AF.Exp, [rsb], [rsb], scale=-0.5)
                        scl = (128.0 ** -0.5) if f == 0 else 1.0
                        stt(ob[:], sil[:, 3:3 + T], scl, rsb[:, 3:3 + T], ALU.mult, ALU.mult, [sil, rsb], [ob])
                        dst = s_qT if f == 0 else s_kT
                        P.dma("sp", dst[head * 128:(head + 1) * 128, :], ob[:], reads=[ob])

        def fam_tokmajor(c0, ncol, dst, func):
            for wt in range(ncol // 512):
                W = nextw()
                wload(W, wl[:, c0 + wt * 512:c0 + (wt + 1) * 512], 16, 512)
                for t in range(NT):
                    p_ = nextp()
                    mmA(p_, W, 512, hT, t * 128)
                    zb = Hb[t % 4]
                    actf(zb[:, 0:512], p_[:, :], func, [p_], [zb])
                    P.dma("sp", dst[t * 128:(t + 1) * 128, wt * 512:(wt + 1) * 512], zb[:, 0:512], reads=[zb])
        fam_tokmajor(C_ZA, 1024, s_za, AF.Silu)
        fam_tokmajor(C_VC, 1024, s_gv, AF.Copy)
        fam_tokmajor(C_ZC, 1024, s_gz, AF.Silu)

        P.dma("pool", wsm[:], wl[:, C_BETA:C_BETA + 16].rearrange("(kc p) n -> p kc n", p=128), writes=[wsm])
        alog = small[:, 0:8]; dtb = small[:, 8:16]; negA = small[:, 16:24]; tmp8 = small[:, 24:32]
        P.dma("sp", alog, gdn_a_log[l, :].partition_broadcast(128), writes=[small])
        P.dma("sp", dtb, gdn_dt_bias[l, :].partition_broadcast(128), writes=[small])
        actf(negA, alog, AF.Exp, [small], [small])
        ts1(negA, negA, -1.0, ALU.mult, [small], [small])
        for t in range(NT):
            p_ = nextp()
            mmA(p_, wsm, 16, hT, t * 128)
            actf(betaS[:, t, :], p_[:, 0:8], AF.Sigmoid, [p_], [betaS])
            tt(tmp8, p_[:, 8:16], dtb, ALU.add, [p_, small], [small])
            actf(tmp8, tmp8, AF.Exp, [small], [small])
            actf(tmp8, tmp8, AF.Ln, [small], [small], bias=1.0)
            tt(gS[:, t, :], tmp8, negA, ALU.mult, [small], [gS])

        cneg = small[:, 32:40]
        actf(cneg, lv[:, 3, :], AF.Exp, [lv], [small], scale=-1.0)
        actf(cneg, cneg, AF.Ln, [small], [small], bias=1.0)
        ts1(cneg, cneg, -8.0, ALU.mult, [small], [small])
        for j in range(2):
            Wx = nextw(); wload(Wx, wl[:, C_XB + j * 512:C_XB + (j + 1) * 512], 16, 512)
            Wy = nextw(); wload(Wy, wl[:, C_YB + j * 512:C_YB + (j + 1) * 512], 16, 512)
            for q in range(4):
                kb = j * 4 + q
                xraw, yraw, xl, r_, ig, a2 = Fb
                xlb = Hb[0]; OB = Hb[1 + kb % 2]
                P.dma("pool", wgi[0][:], lru_w_a[l, kb], writes=[wgi[0]])
                P.dma("pool", wgi[1][:], lru_w_i[l, kb], writes=[wgi[1]])
                proj_rows(xraw, Wx, q * 128)
                proj_rows(yraw, Wy, q * 128)
                fill(2)
                conv4(xl, xraw, cwl[:, kb, :], bias=lv[:, 0, kb:kb + 1])
                cpy("act", xlb[:], xl[:, 3:3 + T], [xl], [xlb])
                for tb in range(4):
                    sl = slice(3 + tb * 512, 3 + (tb + 1) * 512)
                    p_ = nextp()
                    P.op("pe", lambda e, p_=p_, tb=tb: e.matmul(p_[:, :], lhsT=wgi[0][:], rhs=xlb[:, tb * 512:(tb + 1) * 512], start=True, stop=True),
                         reads=[wgi[0], xlb], writes=[p_])
                    actf(r_[:, sl], p_[:, :], AF.Sigmoid, [p_, lv], [r_], bias=lv[:, 1, kb:kb + 1])
                    p2 = nextp()
                    P.op("pe", lambda e, p2=p2, tb=tb: e.matmul(p2[:, :], lhsT=wgi[1][:], rhs=xlb[:, tb * 512:(tb + 1) * 512], start=True, stop=True),
                         reads=[wgi[1], xlb], writes=[p2])
                    actf(ig[:, sl], p2[:, :], AF.Sigmoid, [p2, lv], [ig], bias=lv[:, 2, kb:kb + 1])
                V = slice(3, 3 + T)
                actf(r_[:, V], r_[:, V], AF.Exp, [r_, small], [r_], scale=cneg[:, kb:kb + 1])
                tt(a2[:, V], r_[:, V], r_[:, V], ALU.mult, [r_], [a2])
                actf(a2[:, V], a2[:, V], AF.Sqrt, [a2], [a2], scale=-1.0, bias=1.0)
                mset(a2[:, 3:4], 1.0, [a2])
                tt(ig[:, V], ig[:, V], a2[:, V], ALU.mult, [ig, a2], [ig])
                tt(ig[:, V], ig[:, V], xl[:, V], ALU.mult, [ig, xl], [ig])
                P.op("dve", lambda e, a2=a2, r_=r_, ig=ig: e.tensor_tensor_scan(out=a2[:, 3:3 + T], data0=r_[:, 3:3 + T], data1=ig[:, 3:3 + T],
                                                                                 initial=0.0, op0=ALU.mult, op1=ALU.add),
                     reads=[r_, ig], writes=[a2])
                actf(xl[:, V], yraw[:, V], AF.Square, [yraw], [xl])
                ts(xl[:, V], xl[:, V], 0.044715, 1.0, ALU.mult, ALU.add, [xl], [xl])
                tt(xl[:, V], xl[:, V], yraw[:, V], ALU.mult, [xl, yraw], [xl])
                actf(xl[:, V], xl[:, V], AF.Sigmoid, [xl], [xl], scale=1.5957691216057308)
                tt(xl[:, V], xl[:, V], yraw[:, V], ALU.mult, [xl, yraw], [xl])
                tt(OB[:], xl[:, V], a2[:, V], ALU.mult, [xl, a2], [OB])
                P.dma("sp", s_oT[1, kb * 128:(kb + 1) * 128, :], OB[:], reads=[OB])

        gkT = Fb[5]; wg = k.sb(st, [16, 512], F32, "wg"); nbg = small[:, 40:44]
        P.dma("pool", wsm[:], wl[:, C_GK:C_GK + 16].rearrange("(kc p) n -> p kc n", p=128), writes=[wsm])
        P.dma("sp", wg[:], gla_w_gate[l], writes=[wg])
        P.dma("sp", nbg, gla_b_gate_t[l], writes=[small])
        ts1(nbg, nbg, -1.0, ALU.mult, [small], [small])
        for tb in range(4):
            p_ = nextp()
            mmB(p_, 16, wsm, 0, hT, tb * 512)
            cpy("act", gkT[0:16, tb * 512:(tb + 1) * 512], p_[0:16, :], [p_], [gkT])
        resetm = Fb[4]
        mset(resetm[:, 0:T], 1.0, [resetm])
        mset(resetm[:, 0:T].rearrange("p (c i) -> p c i", i=128)[:, :, 0:1], 0.0, [resetm])
        Wq = nextw(); wload(Wq, wl[:, C_QC:C_QC + 512], 16, 512)
        Wk = nextw(); wload(Wk, wl[:, C_KC:C_KC + 512], 16, 512)
        for h in range(4):
            sp_, bp, eb, enb = Fb[0], Fb[1], Fb[2], Fb[3]
            for tb in range(4):
                p_ = nextp()
                P.op("pe", lambda e, p_=p_, tb=tb, h=h: e.matmul(p_[:, :], lhsT=wg[0:16, h * 128:(h + 1) * 128], rhs=gkT[0:16, tb * 512:(tb + 1) * 512],
                                                                 start=True, stop=True), reads=[wg, gkT], writes=[p_])
                actf(sp_[:, tb * 512:(tb + 1) * 512], p_[:, :], AF.Exp, [p_, small], [sp_], scale=-1.0, bias=nbg[:, h:h + 1])
            actf(sp_[:, 0:T], sp_[:, 0:T], AF.Ln, [sp_], [sp_], bias=1.0)
            fill(2)
            P.op("dve", lambda e, bp=bp, sp_=sp_: e.tensor_tensor_scan(out=bp[:, 0:T], data0=resetm[:, 0:T], data1=sp_[:, 0:T], initial=0.0,
                                                                       op0=ALU.mult, op1=ALU.add), reads=[resetm, sp_], writes=[bp])
            actf(eb[:, 0:T], bp[:, 0:T], AF.Exp, [bp], [eb], scale=-1.0 / 16)
            actf(enb[:, 0:T], bp[:, 0:T], AF.Exp, [bp], [enb], scale=1.0 / 16)
            bp3 = bp[:, 0:T].rearrange("p (c i) -> p c i", i=128)
            tt(sp_[:, 0:T].rearrange("p (c i) -> p c i", i=128), bp3, bp3[:, :, 127:128].to_broadcast([128, NT, 128]), ALU.subtract, [bp], [sp_])
            actf(sp_[:, 0:T], sp_[:, 0:T], AF.Exp, [sp_], [sp_], scale=1.0 / 16)
            cpy("dve", dcS[:, h, :], eb[:, 0:T].rearrange("p (c i) -> p c i", i=128)[:, :, 127], [eb], [dcS])
            QB, KI, KO = Hb[0], Hb[1], Hb[2]
            for tb in range(4):
                sl = slice(tb * 512, (tb + 1) * 512)
                p_ = nextp()
                mmB(p_, 128, Wq, h * 128, hT, tb * 512)
                stt(QB[:, sl], p_[:, :], 128.0 ** -0.5, eb[:, sl], ALU.mult, ALU.mult, [p_, eb], [QB])
                p2 = nextp()
                mmB(p2, 128, Wk, h * 128, hT, tb * 512)
                tt(KI[:, sl], p2[:, :], enb[:, sl], ALU.mult, [p2, enb], [KI])
                tt(KO[:, sl], p2[:, :], sp_[:, sl], ALU.mult, [p2, sp_], [KO])
            P.dma("sp", s_gq[h * 128:(h + 1) * 128, :], QB[:], reads=[QB])
            P.dma("sp", s_gki[h * 128:(h + 1) * 128, :], KI[:], reads=[KI])
            P.dma("sp", s_gko[h * 128:(h + 1) * 128, :], KO[:], reads=[KO])

        fill(48)

    def head_rs(o2, ss, rs, nh, inv_n):
        P.op("dve", lambda e: e.tensor_reduce(out=ss[:, 0:nh], in_=o2, axis=AX.X, op=ALU.add), reads=[o2b[0]], writes=[ss])
        ts(rs[:, 0:nh], ss[:, 0:nh], inv_n, EPS, ALU.mult, ALU.add, [ss], [rs])
        actf(rs[:, 0:nh], rs[:, 0:nh], AF.Sqrt, [rs], [rs])
        P.op("dve", lambda e: e.reciprocal(out=rs[:, 0:nh], in_=rs[:, 0:nh]), reads=[rs], writes=[rs])
    o2b = [None]

    def phase_gdn(l):
        st = k.phase_begin()
        H = 4
        W_ = H * 128

        def V3(ap):
            return ap.rearrange("p (h i) -> p h i", h=H)
        qB = [k.sb(st, [128, 8, 512], BF16, "qB%d" % i) for i in range(2)]
        kB = [k.sb(st, [128, 8, 512], BF16, "kB%d" % i) for i in range(2)]
        vB = [k.sb(st, [128, 8, 512], BF16, "vB%d" % i) for i in range(2)]
        zt = [k.sb(st, [128, 1024], BF16, "zt%d" % i) for i in range(2)]
        gnr = k.sb(st, [128, 128], F32, "gnr")
        zgb = [k.sb(st, [128, 1024], F32, "zg%d" % i) for i in range(2)]
        P.dma("sp", gnr[:], gdn_norm[l, :].partition_broadcast(128), writes=[gnr])

        def bc_i(ap8):
            return ap8.unsqueeze(2).to_broadcast([128, H, 128])

        def bc_h(ap128):
            return ap128.unsqueeze(1).to_broadcast([128, H, 128])

        class Grp:
            pass
        groups = []
        for g in range(2):
            G = Grp(); G.g = g
            G.f = {n: k.sb(st, [128, W_], F32, "%s%d" % (n, g)) for n in ("GU", "Dm", "DU", "DL", "gamT", "gbL", "egr", "u", "S", "o2", "on")}
            G.b = {n: k.sb(st, [128, W_], BF16, "%s%d" % (n, g)) for n in ("kgb", "ktail", "vb", "Bn", "Aqk", "qd", "Xa", "Xb", "Ya", "Yb", "Pm", "wT", "vn", "Sbf", "oa", "oTs")}
            G.sm = k.sb(st, [128, 64], F32, "sm%d" % g)
            G.ssv = k.sb(st, [128, 8], F32, "ssv%d" % g); G.rsv = k.sb(st, [128, 8], F32, "rsv%d" % g)
            G.ps = [k.ps(st, [128, 512], F32, "psb%d_%d" % (g, i)) for i in range(4)]
            G.np = 0
            G.h2 = [{n: k.sb(st, [128, W_], BF16, "%s%d_%d" % (n, g, par)) for n in ("ktail", "vb", "Aqk", "qd", "wT", "Yf")} for par in range(2)]
            G.cd = [k.sb(st, [128, 8], F32, "cd%d_%d" % (g, par)) for par in range(2)]
            mset(G.f["S"][:], 0.0, [G.f["S"]]); mset(G.b["Sbf"][:], 0.0, [G.b["Sbf"]])
            groups.append(G)

        def prep_gen(t, G, q_, k_, v_):
            g = G.g; h0 = g * H
            f32b, sm = G.f, G.sm
            bfb = dict(G.b); bfb.update({n: G.h2[t % 2][n] for n in ("ktail", "vb", "Aqk", "qd", "wT")})
            Yf = G.h2[t % 2]["Yf"]; cdb = G.cd[t % 2]

            def nextp():
                G.np += 1
                return G.ps[G.np % 4]
            tin = t % 4
            cs = slice(tin * 128, (tin + 1) * 128)
            g_t = gS[:, t, h0:h0 + H]; b_t = betaS[:, t, h0:h0 + H]
            S, Sbf = f32b["S"], bfb["Sbf"]
            gcc = sm[:, 0:8]; egc = sm[:, 8:12]; cdec = cdb[:, 0:H]; dcol = sm[:, 16:20]; fk = sm[:, 20:24]; nbeta = sm[:, 24:28]
            GU, Dm, DU, DL, gamT, gbL, egr, u_ = (f32b[n] for n in ("GU", "Dm", "DU", "DL", "gamT", "gbL", "egr", "u"))
            tt(V3(GU[:]), bc_h(CF(1)), bc_i(g_t), ALU.mult, [cF, gS], [GU])
            pg = nextp()
            P.op("pe", lambda e: e.matmul(pg[:, :], lhsT=CF(11), rhs=GU[:], start=True, stop=True), reads=[cF, GU], writes=[pg])
            psm = nextp()

            def smm(e):
                e.matmul(psm[:, 0:H], lhsT=CF(1), rhs=g_t, start=True, stop=True)
                return e.matmul(psm[:, H:2 * H], lhsT=CF(11), rhs=g_t, start=True, stop=True)
            P.op("pe", smm, reads=[cF, gS], writes=[psm])
            cpy("dve", gcc, psm[:, 0:2 * H], [psm], [sm])
            actf(egc, sm[:, 0:H], AF.Exp, [sm], [sm])
            actf(cdec, sm[:, H:2 * H], AF.Exp, [sm], [cdb])
            tt(dcol, sm[:, H:2 * H], sm[:, 0:H], ALU.subtract, [sm], [sm])
            actf(dcol, dcol, AF.Exp, [sm], [sm])
            tt(fk, egc, b_t, ALU.mult, [sm, betaS], [sm])
            ts1(nbeta, b_t, -1.0, ALU.mult, [betaS], [sm])
            tt(V3(Dm[:]), V3(pg[:, :]), bc_i(sm[:, 0:H]), ALU.subtract, [pg, sm], [Dm])
            actf(egr[:], pg[:, :], AF.Exp, [pg], [egr])
            yield
            tt(V3(DU[:]), V3(Dm[:]), bc_h(CF(2)), ALU.add, [Dm, cF], [DU])
            actf(gamT[:], DU[:], AF.Exp, [DU], [gamT])
            tt(V3(DL[:]), V3(Dm[:]), bc_h(CF(3)), ALU.add, [Dm, cF], [DL])
            actf(DL[:], DL[:], AF.Exp, [DL], [DL], scale=-1.0)
            tt(V3(gbL[:]), V3(DL[:]), bc_i(nbeta), ALU.mult, [DL, sm], [gbL])
            pk = nextp(); pkv = pk[:].bitcast(BF16)[:, 0:W_]

            def trk(e):
                for h in range(H):
                    ins = e.transpose(out=pkv[:, h * 128:(h + 1) * 128], in_=k_[:, h0 + h, cs], identity=CB(0))
                return ins
            P.op("pe", trk, reads=[k_, cB], writes=[pk])
            tt(V3(bfb["kgb"][:]), V3(pkv), bc_i(fk), ALU.mult, [pk, sm], [bfb["kgb"]])
            tt(V3(bfb["ktail"][:]), V3(pkv), bc_i(dcol), ALU.mult, [pk, sm], [bfb["ktail"]])
            pv = nextp(); pvv = pv[:].bitcast(BF16)[:, 0:W_]

            def trv(e):
                for h in range(H):
                    ins = e.transpose(out=pvv[:, h * 128:(h + 1) * 128], in_=v_[:, h0 + h, cs], identity=CB(0))
                return ins
            P.op("pe", trv, reads=[v_, cB], writes=[pv])
            tt(V3(bfb["vb"][:]), V3(pvv), bc_i(b_t), ALU.mult, [pv, betaS], [bfb["vb"]])
            yield
            pG = nextp()

            def gmm(e):
                for h in range(H):
                    ins = e.matmul(pG[:, h * 128:(h + 1) * 128], lhsT=k_[:, h0 + h, cs], rhs=k_[:, h0 + h, cs], start=True, stop=True)
                return ins
            P.op("pe", gmm, reads=[k_], writes=[pG])
            Bn = bfb["Bn"]
            tt(Bn[:], pG[:, :], gbL[:], ALU.mult, [pG, gbL], [Bn])
            pA = nextp()

            def amm(e):
                for h in range(H):
                    ins = e.matmul(pA[:, h * 128:(h + 1) * 128], lhsT=k_[:, h0 + h, cs], rhs=q_[:, h0 + h, cs], start=True, stop=True)
                return ins
            P.op("pe", amm, reads=[k_, q_], writes=[pA])
            Aqk = bfb["Aqk"]
            tt(Aqk[:], pA[:, :], gamT[:], ALU.mult, [pA, gamT], [Aqk])
            qd = bfb["qd"]
            tt(V3(qd[:]), q_[:, h0:h0 + H, cs], V3(egr[:]), ALU.mult, [q_, egr], [qd])
            yield
            X, Xo, Y, Yo, Pm = bfb["Xa"], bfb["Xb"], bfb["Ya"], bfb["Yb"], bfb["Pm"]
            tt(V3(X[:]), V3(Bn[:]), bc_h(CB(4)), ALU.mult, [Bn, cB], [X])
            tt(V3(X[:]), V3(X[:]), bc_h(CB(0)), ALU.add, [X, cB], [X])
            py = nextp(); pyv = py[:].bitcast(BF16)[:, 0:W_]

            def try_(e, X=X):
                for h in range(H):
                    ins = e.transpose(out=pyv[:, h * 128:(h + 1) * 128], in_=X[:, h * 128:(h + 1) * 128], identity=CB(0))
                return ins
            P.op("pe", try_, reads=[X, cB], writes=[py])
            cpy("act", Y[:], pyv, [py], [Y])
            yield
            for lev in range(2, 8):
                pP = nextp()

                def pmm(e, pP=pP, Y=Y):
                    for h in range(H):
                        hs = slice(h * 128, (h + 1) * 128)
                        ins = e.matmul(pP[:, hs], lhsT=Bn[:, hs], rhs=Y[:, hs], start=True, stop=True)
                    return ins
                P.op("pe", pmm, reads=[Bn, Y], writes=[pP])
                tt(V3(Pm[:]), V3(pP[:, :]), bc_h(CF(3 + lev)), ALU.mult, [pP, cF], [Pm])
                yield
                pYn = nextp()
                if lev == 7:
                    Yo = Yf

                def ymm(e, pYn=pYn, X=X, Y=Y):
                    for h in range(H):
                        hs = slice(h * 128, (h + 1) * 128)
                        e.matmul(pYn[:, hs], lhsT=CB(0), rhs=Y[:, hs], start=True, stop=False)
                        ins = e.matmul(pYn[:, hs], lhsT=X[:, hs], rhs=Pm[:, hs], start=False, stop=True)
                    return ins
                P.op("pe", ymm, reads=[X, Pm, Y, cB], writes=[pYn])
                cpy("act", Yo[:], pYn[:, :], [pYn], [Yo])
                if lev < 7:
                    pXn = nextp()

                    def xmm(e, pXn=pXn, X=X):
                        for h in range(H):
                            hs = slice(h * 128, (h + 1) * 128)
                            e.matmul(pXn[:, hs], lhsT=CB(0), rhs=X[:, hs], start=True, stop=False)
                            ins = e.matmul(pXn[:, hs], lhsT=Pm[:, hs], rhs=X[:, hs], start=False, stop=True)
                        return ins
                    P.op("pe", xmm, reads=[X, Pm, cB], writes=[pXn])
                    cpy("act", Xo[:], pXn[:, :], [pXn], [Xo])
                    X, Xo = Xo, X
                Y, Yo = Yo, Y
                yield
            pw = nextp()

            def wmm(e, Y=Y):
                for h in range(H):
                    hs = slice(h * 128, (h + 1) * 128)
                    ins = e.matmul(pw[:, hs], lhsT=bfb["kgb"][:, hs], rhs=Y[:, hs], start=True, stop=True)
                return ins
            P.op("pe", wmm, reads=[Y, bfb["kgb"]], writes=[pw])
            wT = bfb["wT"]
            P.op("act", lambda e: e.mul(out=wT[:], in_=pw[:, :], mul=-1.0), reads=[pw], writes=[wT])

        def recur_gen(t, G, zg):
            g = G.g; h0 = g * H
            f32b, sm = G.f, G.sm
            bfb = dict(G.b); bfb.update({n: G.h2[t % 2][n] for n in ("ktail", "vb", "Aqk", "qd", "wT")})
            Y = G.h2[t % 2]["Yf"]; cdec = G.cd[t % 2][:, 0:H]; cdb = G.cd[t % 2]
            wT, qd, Aqk = bfb["wT"], bfb["qd"], bfb["Aqk"]
            S, Sbf = f32b["S"], bfb["Sbf"]

            def nextp():
                G.np += 1
                return G.ps[G.np % 4]
            pws = nextp()

            def wsmm(e, Y=Y):
                for h in range(H):
                    hs = slice(h * 128, (h + 1) * 128)
                    e.matmul(pws[:, hs], lhsT=Y[:, hs], rhs=bfb["vb"][:, hs], start=True, stop=False)
                    ins = e.matmul(pws[:, hs], lhsT=wT[:, hs], rhs=Sbf[:, hs], start=False, stop=True)
                return ins
            P.op("pe", wsmm, reads=[wT, Sbf, Y, bfb["vb"]], writes=[pws])
            vn = bfb["vn"]
            cpy("act", vn[:], pws[:, :], [pws], [vn])
            yield
            po = nextp()

            def omm(e):
                for h in range(H):
                    hs = slice(h * 128, (h + 1) * 128)
                    e.matmul(po[:, hs], lhsT=qd[:, hs], rhs=Sbf[:, hs], start=True, stop=False)
                    ins = e.matmul(po[:, hs], lhsT=Aqk[:, hs], rhs=vn[:, hs], start=False, stop=True)
                return ins
            P.op("pe", omm, reads=[qd, Sbf, Aqk, vn], writes=[po])
            pkv2 = nextp()

            def kvmm(e):
                for h in range(H):
                    hs = slice(h * 128, (h + 1) * 128)
                    ins = e.matmul(pkv2[:, hs], lhsT=bfb["ktail"][:, hs], rhs=vn[:, hs], start=True, stop=True)
                return ins
            P.op("pe", kvmm, reads=[bfb["ktail"], vn], writes=[pkv2])
            tt(V3(S[:]), V3(S[:]), bc_i(cdec), ALU.mult, [S, cdb], [S])
            tt(S[:], S[:], pkv2[:, :], ALU.add, [S, pkv2], [S])
            cpy("act", Sbf[:], S[:], [S], [Sbf])
            o2, on, oa, oTs = f32b["o2"], f32b["on"], bfb["oa"], bfb["oTs"]
            actf(o2[:], po[:, :], AF.Square, [po], [o2])
            o2b[0] = o2
            head_rs(V3(o2[:]), G.ssv, G.rsv, H, 1.0 / 128)
            for h in range(H):
                hs = slice(h * 128, (h + 1) * 128)
                actf(on[:, hs], po[:, hs], AF.Identity, [po, G.rsv], [on], scale=G.rsv[:, h:h + 1])
            tt(oa[:], on[:], zg[:, h0 * 128:(h0 + H) * 128], ALU.mult, [on, zg], [oa])
            yield
            pot = nextp(); potv = pot[:].bitcast(BF16)[:, 0:W_]

            def tro(e):
                for h in range(H):
                    hs = slice(h * 128, (h + 1) * 128)
                    ins = e.transpose(out=potv[:, hs], in_=oa[:, hs], identity=CB(0))
                return ins
            P.op("pe", tro, reads=[oa, cB], writes=[pot])
            cpy("act", oTs[:], potv, [pot], [oTs])
            P.dma("sp", s_oT[0].rearrange("(h p) t -> p h t", p=128)[:, h0:h0 + H, t * 128:(t + 1) * 128], V3(oTs[:]), reads=[oTs])

        def loads(t):
            blk, tin = t // 4, t % 4
            q_, k_, v_ = qB[blk % 2], kB[blk % 2], vB[blk % 2]
            if tin == 0:
                for dstb, src in ((q_, s_qT), (k_, s_kT), (v_, s_vT)):
                    P.dma("sp", dstb[:], src.rearrange("(h p) t -> p h t", p=128)[:, :, blk * 512:(blk + 1) * 512], writes=[dstb])
            z_ = zt[t % 2]
            P.dma("sp", z_[:], s_za[t * 128:(t + 1) * 128, :], writes=[z_])
            zg = zgb[t % 2]
            P.op("pool", lambda e, zg=zg, z_=z_: e.tensor_tensor(out=zg[:].rearrange("p (h i) -> p h i", h=8), in0=z_[:].rearrange("p (h i) -> p h i", h=8),
                                                               in1=gnr[:].unsqueeze(1).to_broadcast([128, 8, 128]), op=ALU.mult), reads=[z_, gnr], writes=[zg])
            return q_, k_, v_, zg

        def rr(gens):
            alive = list(gens)
            while alive:
                for gn in list(alive):
                    try:
                        next(gn)
                    except StopIteration:
                        alive.remove(gn)
        q_, k_, v_, zg_cur = loads(0)
        rr([prep_gen(0, G, q_, k_, v_) for G in groups])
        for t in range(NT):
            gens = [recur_gen(t, G, zg_cur) for G in groups]
            if t + 1 < NT:
                q_, k_, v_, zg_next = loads(t + 1)
                gens += [prep_gen(t + 1, G, q_, k_, v_) for G in groups]
            rr(gens)
            if t + 1 < NT:
                zg_cur = zg_next
        k.phase_end(st)

    def phase_gla(l, lmod=None):
        st = k.phase_begin()
        H = 4
        filler = mod_gen(lmod, st, 2) if lmod is not None else None
        qB = [k.sb(st, [128, H, 512], BF16, "gqB%d" % i) for i in range(2)]
        kiB = [k.sb(st, [128, H, 512], BF16, "gkiB%d" % i) for i in range(2)]
        koB = [k.sb(st, [128, H, 512], BF16, "gkoB%d" % i) for i in range(2)]
        vt = [k.sb(st, [128, 1024], BF16, "gvt%d" % i) for i in range(2)]
        zt = [k.sb(st, [128, 1024], BF16, "gzt%d" % i) for i in range(2)]
        S = k.sb(st, [128, 1024], F32, "gS_"); Sbf = k.sb(st, [128, 1024], BF16, "gSbf")
        o2 = k.sb(st, [128, 1024], F32, "go2"); on = k.sb(st, [128, 1024], F32, "gon")
        kot = k.sb(st, [128, 512], BF16, "kot"); AT = k.sb(st, [128, 512], BF16, "gAT")
        oc = k.sb(st, [128, 1024], BF16, "goc"); oTs = k.sb(st, [128, 1024], BF16, "goTs")
        gnr = k.sb(st, [128, 256], F32, "ggnr")
        ssv = k.sb(st, [128, 8], F32, "gssv"); rsv = k.sb(st, [128, 8], F32, "grsv")
        psb = [k.ps(st, [128, 1024], F32, "gps%d" % i) for i in range(3)]
        nP = [0]

        def nextp():
            nP[0] += 1
            return psb[nP[0] % 3]
        P.dma("sp", gnr[:], gla_norm[l, :].partition_broadcast(128), writes=[gnr])
        mset(S[:], 0.0, [S]); mset(Sbf[:], 0.0, [Sbf])
        o2b[0] = o2

        def V4(ap):
            return ap.rearrange("p (h i) -> p h i", h=H)
        for t in range(NT):
            blk, tin = t // 4, t % 4
            q_, ki_, ko_ = qB[blk % 2], kiB[blk % 2], koB[blk % 2]
            if tin == 0:
                for dstb, src in ((q_, s_gq), (ki_, s_gki), (ko_, s_gko)):
                    P.dma("sp", dstb[:], src.rearrange("(h p) t -> p h t", p=128)[:, :, blk * 512:(blk + 1) * 512], writes=[dstb])
            v_, z_ = vt[t % 2], zt[t % 2]
            P.dma("sp", v_[:], s_gv[t * 128:(t + 1) * 128, :], writes=[v_])
            P.dma("sp", z_[:], s_gz[t * 128:(t + 1) * 128, :], writes=[z_])
            cs = slice(tin * 128, (tin + 1) * 128)
            pk = nextp(); pkv = pk[:].bitcast(BF16)[:, 0:512]

            def trk(e, pkv=pkv, ko_=ko_, cs=cs):
                for h in range(H):
                    ins = e.transpose(out=pkv[:, h * 128:(h + 1) * 128], in_=ko_[:, h, cs], identity=CB(0))
                return ins
            P.op("pe", trk, reads=[ko_, cB], writes=[pk])
            cpy("act", kot[:], pkv, [pk], [kot])
            pA = nextp()

            def amm(e, pA=pA, ki_=ki_, q_=q_, cs=cs):
                for h in range(H):
                    ins = e.matmul(pA[:, h * 128:(h + 1) * 128], lhsT=ki_[:, h, cs], rhs=q_[:, h, cs], start=True, stop=True)
                return ins
            P.op("pe", amm, reads=[ki_, q_], writes=[pA])
            tt(V4(AT[:]), V4(pA[:, 0:512]), CF(1).unsqueeze(1).to_broadcast([128, H, 128]), ALU.mult, [pA, cF], [AT])
            po = nextp()

            def omm(e, po=po, q_=q_, v_=v_, cs=cs):
                for h in range(H):
                    e.matmul(po[:, h * 256:(h + 1) * 256], lhsT=q_[:, h, cs], rhs=Sbf[:, h * 256:(h + 1) * 256], start=True, stop=False)
                    ins = e.matmul(po[:, h * 256:(h + 1) * 256], lhsT=AT[:, h * 128:(h + 1) * 128], rhs=v_[:, h * 256:(h + 1) * 256], start=False, stop=True)
                return ins
            P.op("pe", omm, reads=[q_, Sbf, AT, v_], writes=[po])
            pkv2 = nextp()

            def kvmm(e, pkv2=pkv2, v_=v_):
                for h in range(H):
                    ins = e.matmul(pkv2[:, h * 256:(h + 1) * 256], lhsT=kot[:, h * 128:(h + 1) * 128], rhs=v_[:, h * 256:(h + 1) * 256], start=True, stop=True)
                return ins
            P.op("pe", kvmm, reads=[kot, v_], writes=[pkv2])
            S4 = S[:].rearrange("p (h i) -> p h i", h=H)
            tt(S4, S4, dcS[:, :, t].unsqueeze(2).to_broadcast([128, H, 256]), ALU.mult, [S, dcS], [S])
            tt(S[:], S[:], pkv2[:, :], ALU.add, [S, pkv2], [S])
            cpy("act", Sbf[:], S[:], [S], [Sbf])
            actf(o2[:], po[:, :], AF.Square, [po], [o2])
            head_rs(V4(o2[:]), ssv, rsv, 4, 1.0 / 256)
            tt(V4(on[:]), V4(po[:, :]), rsv[:, 0:4].unsqueeze(2).to_broadcast([128, H, 256]), ALU.mult, [po, rsv], [on])
            tt(V4(on[:]), V4(on[:]), gnr[:].unsqueeze(1).to_broadcast([128, H, 256]), ALU.mult, [on, gnr], [on])
            tt(oc[:], on[:], z_[:], ALU.mult, [on, z_], [oc])
            pot = nextp(); potv = pot[:].bitcast(BF16)[:, 0:1024]

            def tro(e, potv=potv):
                for m in range(8):
                    ms = slice(m * 128, (m + 1) * 128)
                    ins = e.transpose(out=potv[:, ms], in_=oc[:, ms], identity=CB(0))
                return ins
            P.op("pe", tro, reads=[oc, cB], writes=[pot])
            cpy("act", oTs[:], potv, [pot], [oTs])
            P.dma("sp", s_oT[2].rearrange("(m p) t -> p m t", p=128)[:, :, t * 128:(t + 1) * 128],
                  oTs[:].rearrange("p (m i) -> p m i", m=8), reads=[oTs])
            if filler is not None:
                for _ in range(2):
                    try:
                        next(filler)
                    except StopIteration:
                        filler = None
                        break
        if filler is not None:
            for _ in filler:
                pass
        k.phase_end(st)

    class WT:
        def __init__(self, halves):
            self.h = halves

        def ap(self, kc, cols):
            return self.h[kc // 8][:, kc % 8, cols]

    def wt_load(pool, nW, src2d, kcs, scr2d=None, first=True, wscr=None):
        hs = []
        for i in range(kcs // 8):
            b = pool[nW[0] % len(pool)]; nW[0] += 1
            rs_ = slice(i * 1024, (i + 1) * 1024)
            if first or scr2d is None:
                P.dma("pool", b[:], src2d[rs_, :].rearrange("(kc p) n -> p kc n", p=128), writes=[b])
                if scr2d is not None:
                    tok = P.dma("sp", scr2d[rs_, :].rearrange("(kc p) n -> p kc n", p=128), b[:], reads=[b])
                    _merge(wscr.lastw, tok)
            else:
                P.dma("sp", b[:], scr2d[rs_, :].rearrange("(kc p) n -> p kc n", p=128), reads=[wscr], writes=[b])
            hs.append(b)
        return WT(hs)

    def load_col(st, l, i, name):
        c = k.sb(st, [128, 16], F32, name)
        P.dma("sp", c[:], rows[l, i, :].rearrange("(kc p) -> p kc", p=128), writes=[c], allow_slow_non_contiguous=True)
        return c

    def norm_A(x1, junk, hn, ss, rs, nev):
        actf(junk, x1, AF.Square, nev[2], [ss_b(nev)] + nev[1], accum_out=ss)
        ts(rs, ss, 1.0 / D, EPS, ALU.mult, ALU.add, [ss_b(nev)], [rs_b(nev)])
        actf(rs, rs, AF.Sqrt, [rs_b(nev)], [rs_b(nev)])
        P.op("dve", lambda e: e.reciprocal(out=rs, in_=rs), reads=[rs_b(nev)], writes=[rs_b(nev)])
        actf(hn, x1, AF.Identity, nev[2] + [rs_b(nev)], nev[3], scale=rs)

    def ss_b(nev):
        return nev[4]

    def rs_b(nev):
        return nev[5]

    def norm_B(hn, hnb, Ac, Bc, hT, tcol, pT, nev):
        for g4 in range(4):
            p_ = pT[nev[0] % len(pT)]

            def tr(e, g4=g4, p_=p_):
                for j in range(4):
                    kc = g4 * 4 + j
                    ins = e.transpose(out=p_[:, j * 128:(j + 1) * 128], in_=hn[:, kc * 128:(kc + 1) * 128], identity=CF(0))
                return ins
            P.op("pe", tr, reads=[hnb, cF], writes=[p_])
            for j in range(4):
                kc = g4 * 4 + j
                dst = hT[:, kc, tcol:tcol + 128]
                src = p_[:, j * 128:(j + 1) * 128]
                if (kc + nev[0]) % 2 == 0:
                    actf(dst, src, AF.Identity, [p_, Ac, Bc], [hT], scale=Ac[:, kc:kc + 1], bias=Bc[:, kc:kc + 1])
                else:
                    ts(dst, src, Ac[:, kc:kc + 1], Bc[:, kc:kc + 1], ALU.mult, ALU.add, [p_, Ac, Bc], [hT])
            nev[0] += 1

    def phase_norm1(l, xsrc, st, hT):
        A1 = load_col(st, l, 0, "A1c"); B1 = load_col(st, l, 1, "B1c")
        xt = [k.sb(st, [128, D], F32, "xt%d" % i) for i in range(2)]
        junk = k.sb(st, [128, D], BF16, "junk")
        hn = [k.sb(st, [128, D], F32, "hn%d" % i) for i in range(2)]
        ss = [k.sb(st, [128, 1], F32, "ss%d" % i) for i in range(2)]
        rs = [k.sb(st, [128, 1], F32, "rs%d" % i) for i in range(2)]
        pT = [k.ps(st, [128, 512], F32, "pT%d" % i) for i in range(6)]
        cnt = [0]

        def A(t):
            x_ = xt[t % 2]
            P.dma("sp", x_[:], xsrc[t * 128:(t + 1) * 128, :], writes=[x_])
            nev = [0, [junk], [x_], [hn[t % 2]], ss[t % 2], rs[t % 2]]
            norm_A(x_[:], junk[:], hn[t % 2][:], ss[t % 2][:], rs[t % 2][:], nev)

        def B(t):
            nev = [cnt[0]]
            norm_B(hn[t % 2], hn[t % 2], A1, B1, hT, t * 128, pT, nev)
            cnt[0] = nev[0]
        A(0)
        for t in range(NT):
            if t + 1 < NT:
                A(t + 1)
            B(t)

    def phase_post(l, xsrc, xdst):
        st = k.phase_begin()
        big3 = k.sb(st, [128, 4, D], F32, "big3")
        U16 = k.sb(st, [128, 16, 512], BF16, "U16")
        oTb = k.sb(st, [128, 24, 512], BF16, "oTb")
        gtb = [k.sb(st, [128, 3, 512], BF16, "gt%d" % i) for i in range(2)]
        mf = [k.sb(st, [128, 512], F32, "mf%d" % i) for i in range(2)]
        y4 = [k.sb(st, [128, D], F32, "y4_%d" % i) for i in range(4)]
        xtb = k.sb(st, [128, D], F32, "xtb")
        rA = k.sb(st, [128, D], F32, "rA")
        h2T = k.sb(st, [128, 16, 512], BF16, "h2T")
        wh = [k.sb(st, [128, 8, 512], BF16, "wh%d" % i) for i in range(5)]
        rG2 = k.sb(st, [128, D], F32, "rG2")
        ss = k.sb(st, [128, 1], F32, "ssp"); rs = k.sb(st, [128, 1], F32, "rsp"); ssr = k.sb(st, [128, 24], F32, "ssr")
        A2 = load_col(st, l, 3, "A2c"); B2 = load_col(st, l, 4, "B2c")
        pp = [k.ps(st, [128, 512], F32, "ppp%d" % i) for i in range(4)]
        pacc = [k.ps(st, [128, 512], F32, "pacc%d" % i) for i in range(4)]
        nW = [0]; nP = [0]; nev = [0, [U16]]
        junk = U16[:].rearrange("p a b -> p (a b)")[:, 0:D]

        def nextp():
            nP[0] += 1
            return pp[nP[0] % 4]

        def load_oT(tb):
            for n in range(3):
                P.dma("sp", oTb[:, n * 8:(n + 1) * 8, :], s_oT[n].rearrange("(kc p) t -> p kc t", p=128)[:, :, tb * 512:(tb + 1) * 512], writes=[oTb])
        wscr = Buf(None)
        load_oT(0)
        P.dma("sp", rA[:], rows[l, 2, :].partition_broadcast(128), writes=[rA])
        P.dma("sp", rG2[:], rows[l, 5, :].partition_broadcast(128), writes=[rG2])
        junk2 = h2T[:].rearrange("p a b -> p (a b)")[:, 0:D]
        m4 = [big3[:, ti, :] for ti in range(4)]

        def f6_stats():
            for ti in range(4):
                actf(junk2, m4[ti], AF.Square, [big3], [h2T, ssr], accum_out=ssr[:, ti:ti + 1])
            ts(ssr[:, 8:12], ssr[:, 0:4], 1.0 / D, EPS, ALU.mult, ALU.add, [ssr], [ssr])
            actf(ssr[:, 8:12], ssr[:, 8:12], AF.Sqrt, [ssr], [ssr])
            P.op("dve", lambda e: e.reciprocal(out=ssr[:, 8:12], in_=ssr[:, 8:12]), reads=[ssr], writes=[ssr])
            ts(ssr[:, 16:20], ssr[:, 8:12], 1.0, None, ALU.mult, ALU.bypass, [ssr], [ssr]) if False else cpy("dve", ssr[:, 16:20], ssr[:, 8:12], [ssr], [ssr])

        def f6_tile(tbp, ti):
            r0 = tbp * 512 + ti * 128
            stt(m4[ti], m4[ti], ssr[:, 16 + ti:17 + ti], rG2[:], ALU.mult, ALU.mult, [big3, ssr, rG2], [big3])
            tt(y4[ti][:], y4[ti][:], m4[ti], ALU.add, [y4[ti], big3], [y4[ti]])
            P.dma("sp", xdst[r0:r0 + 128, :], y4[ti][:], reads=[y4[ti]])
        for tb in range(4):
            tsl = slice(tb * 512, (tb + 1) * 512)
            for c4 in range(4):
                Wn = [wt_load(wh, nW, w_branch[l, n, :, c4 * 512:(c4 + 1) * 512], 8) for n in range(3)]
                for cg in range(4):
                    cga = c4 * 4 + cg
                    g_ = gtb[cga % 2]
                    P.dma("sp", g_[:], s_gates.rearrange("(n r) t -> r n t", n=3)[cga * 128:(cga + 1) * 128, :, tsl], writes=[g_])
                    m_ = mf[cga % 2]
                    for n in range(3):
                        p_ = nextp()

                        def f(e, p_=p_, W=Wn[n], n=n, cg=cg):
                            for kc in range(8):
                                ins = e.matmul(p_[:, :], lhsT=W.ap(kc, slice(cg * 128, (cg + 1) * 128)), rhs=oTb[:, n * 8 + kc, :], start=(kc == 0), stop=(kc == 7))
                            return ins
                        P.op("pe", f, reads=Wn[n].h + [oTb], writes=[p_])
                        if n == 0:
                            tt(m_[:], p_[:, :], g_[:, 0, :], ALU.mult, [p_, g_], [m_])
                        elif n == 1:
                            tt(xtb[:, 0:512], p_[:, :], g_[:, 1, :], ALU.mult, [p_, g_], [xtb])
                            tt(m_[:], m_[:], xtb[:, 0:512], ALU.add, [m_, xtb], [m_])
                        else:
                            tt(xtb[:, 512:1024], p_[:, :], g_[:, 2, :], ALU.mult, [p_, g_], [xtb])
                            tt(U16[:, cga, :], m_[:], xtb[:, 512:1024], ALU.add, [m_, xtb], [U16])
                if tb > 0:
                    f6_tile(tb - 1, c4)
            if tb < 3:
                load_oT(tb + 1)
            for nch in range(4):
                W = wt_load(wh, nW, w_out[l, :, nch * 512:(nch + 1) * 512], 16)
                for ti in range(4):
                    p_ = nextp()

                    def f(e, p_=p_, W=W, ti=ti):
                        for kc in range(16):
                            ins = e.matmul(p_[:, :], lhsT=U16[:, kc, ti * 128:(ti + 1) * 128], rhs=W.ap(kc, slice(0, 512)), start=(kc == 0), stop=(kc == 15))
                        return ins
                    P.op("pe", f, reads=W.h + [U16], writes=[p_])
                    cpy("act" if ti % 2 else "dve", y4[ti][:, nch * 512:(nch + 1) * 512], p_[:, :], [p_], [y4[ti]])
            for ti in range(4):
                y_ = y4[ti]
                actf(junk, y_[:], AF.Square, [y_], [U16, ssr], accum_out=ssr[:, ti:ti + 1])
            ts(ssr[:, 8:12], ssr[:, 0:4], 1.0 / D, EPS, ALU.mult, ALU.add, [ssr], [ssr])
            actf(ssr[:, 8:12], ssr[:, 8:12], AF.Sqrt, [ssr], [ssr])
            P.op("dve", lambda e: e.reciprocal(out=ssr[:, 8:12], in_=ssr[:, 8:12]), reads=[ssr], writes=[ssr])
            for ti in range(4):
                r0 = tb * 512 + ti * 128
                xb_, xa_ = (xtb, xtb[:]) if ti % 2 == 0 else (big3, big3[:, 0, :])
                P.dma("sp", xa_, xsrc[r0:r0 + 128, :], writes=[xb_])
                y_ = y4[ti]
                stt(y_[:], y_[:], ssr[:, 8 + ti:9 + ti], rA[:], ALU.mult, ALU.mult, [y_, ssr, rA], [y_])
                P.op("dve", lambda e, y_=y_, xa_=xa_: e.tensor_tensor(out=y_[:], in0=y_[:], in1=xa_, op=ALU.add), reads=[y_, xb_], writes=[y_])
            for ti in range(4):
                actf(junk, y4[ti][:], AF.Square, [y4[ti]], [U16, ssr], accum_out=ssr[:, 4 + ti:5 + ti])
            ts(ssr[:, 12:16], ssr[:, 4:8], 1.0 / D, EPS, ALU.mult, ALU.add, [ssr], [ssr])
            actf(ssr[:, 12:16], ssr[:, 12:16], AF.Sqrt, [ssr], [ssr])
            P.op("dve", lambda e: e.reciprocal(out=ssr[:, 12:16], in_=ssr[:, 12:16]), reads=[ssr], writes=[ssr])
            for ti in range(4):
                actf(big3[:, ti, :], y4[ti][:], AF.Identity, [y4[ti], ssr], [big3], scale=ssr[:, 12 + ti:13 + ti])
            for ti in range(4):
                nevb = [nev[0]]
                norm_B(big3[:, ti, :], big3, A2, B2, h2T, ti * 128, pp, nevb)
                nev[0] = nevb[0]
            for qq in range(4):
                for c4 in range(4):
                    c0 = qq * 2048 + c4 * 512
                    W = wt_load(wh, nW, w_mlp1[l, :, c0:c0 + 512], 16)
                    for cg in range(4):
                        p_ = nextp()

                        def f(e, p_=p_, W=W, cg=cg):
                            for kc in range(16):
                                ins = e.matmul(p_[:, :], lhsT=W.ap(kc, slice(cg * 128, (cg + 1) * 128)), rhs=h2T[:, kc, :], start=(kc == 0), stop=(kc == 15))
                            return ins
                        P.op("pe", f, reads=W.h + [h2T], writes=[p_])
                        m_ = mf[cg % 2]
                        actf(m_[:], p_[:, :], AF.Square, [p_], [m_])
                        stt(U16[:, c4 * 4 + cg, :], p_[:, :], 0.0, m_[:], ALU.is_gt, ALU.mult, [p_, m_], [U16])
                for nch in range(4):
                    for kq in range(2):
                        r0 = (qq * 16 + kq * 8) * 128
                        W = wt_load(wh, nW, w_mlp2[l, r0:r0 + 1024, nch * 512:(nch + 1) * 512], 8)

                        def f(e, W=W, kq=kq):
                            for kc in range(8):
                                for ti in range(4):
                                    ins = e.matmul(pacc[ti][:, :], lhsT=U16[:, kq * 8 + kc, ti * 128:(ti + 1) * 128], rhs=W.ap(kc, slice(0, 512)),
                                                   start=(kq == 0 and kc == 0), stop=(kq == 1 and kc == 7))
                            return ins
                        P.op("pe", f, reads=W.h + [U16], writes=pacc)
                    for ti in range(4):
                        dst = m4[ti][:, nch * 512:(nch + 1) * 512]
                        if qq == 0:
                            cpy("act" if ti % 2 else "dve", dst, pacc[ti][:, :], [pacc[ti]], [big3])
                        else:
                            tt(dst, pacc[ti][:, :], dst, ALU.add, [pacc[ti], big3], [big3])
            f6_stats()
            if tb == 3:
                for ti in range(4):
                    f6_tile(tb, ti)
        k.phase_end(st)

    for l in range(nlayers):
        xsrc = x_in if l == 0 else xs0
        xdst = xs0 if (l == 0 and nlayers > 1) else out
        if l == 0:
            phase_mod(l)
        if stop == "mod":
            break
        st = k.phase_begin()
        hT = k.sb(st, [128, 16, T], BF16, "hT")
        st1 = ExitStack()
        live_save = k.live
        k.live = []
        phase_norm1(l, xsrc, st1, hT)
        if s_hdbg is not None and l == 0:
            P.dma("sp", s_hdbg.rearrange("(kc p) t -> p kc t", p=128), hT[:], reads=[hT])
        P.barrier(); P.emit(); P.release(k.live); st1.close()
        k.live = live_save
        if stop == "norm1":
            k.phase_end(st)
            break
        phase_inproj(l, st, hT)
        k.phase_end(st)
        if stop == "inproj":
            break
        phase_gdn(l)
        if stop == "gdn":
            break
        phase_gla(l, (l + 1) if (l + 1 < nlayers) else None)
        if stop == "gla":
            break
        phase_post(l, xsrc, xdst)
        P.fresh_engine_sems()

    P.barrier()
    P.emit()
    pst.close()
    k.es.close()
    return k


def host_inputs(inputs, b):
    f = np.float32
    d = {}
    d["x"] = np.ascontiguousarray(inputs["x"][b], dtype=f)
    d["c_t"] = np.ascontiguousarray(np.asarray(inputs["c"][b], dtype=f).reshape(16, 128).T)
    d["consts"] = _consts()
    for n in ("w_ada", "b_ada", "g_pre_mix", "g_post_mix", "g_pre_mlp", "g_post_mlp", "w_in", "gdn_a_log", "gdn_dt_bias",
              "gdn_norm", "lru_w_a", "lru_w_i", "gla_w_gate", "gla_norm", "w_branch", "w_out", "w_mlp1", "w_mlp2"):
        d[n] = np.ascontiguousarray(inputs[n], dtype=f)
    d["conv_gdn_t"] = np.ascontiguousarray(np.asarray(inputs["conv_gdn"], f).reshape(2, 4, 24, 128).transpose(0, 3, 2, 1))
    d["conv_lru_t"] = np.ascontiguousarray(np.asarray(inputs["conv_lru"], f).reshape(2, 4, 8, 128).transpose(0, 3, 2, 1))
    lv = np.stack([np.asarray(inputs[n], f).reshape(2, 8, 128) for n in ("conv_lru_b", "lru_b_a", "lru_b_i", "lru_lambda")], axis=1)
    d["lru_vec_t"] = np.ascontiguousarray(lv.transpose(0, 3, 1, 2))
    d["gla_b_gate_t"] = np.ascontiguousarray(np.asarray(inputs["gla_b_gate"], f).reshape(2, 4, 128).transpose(0, 2, 1))
    return d


_CACHE = {}


def kernel(**inputs):
    if "k" not in _CACHE:
        _CACHE["k"] = build()
    k = _CACHE["k"]
    in_maps = [host_inputs(inputs, b) for b in range(8)]
    res = run_bass_kernel_spmd(k.nc, in_maps, core_ids=list(range(8)))
    return np.stack([np.asarray(r["out"], dtype=np.float32) for r in res.results], axis=0)
```
